# Optimizing a Trainium2 kernel written in Bass

```python
import math
import jax, jax.numpy as jnp
from jax import lax
import numpy as np

D_MODEL = 1024
BATCH = 4
SEQ = 8192
DEPTH = 1

NORM_EPS = 1e-6
NEG_INF = -1e30

CHUNK = 128
GMLP_WIDTH = D_MODEL
GMLP_GROUPS = 8
GMLP_GROUP_DIM = GMLP_WIDTH // GMLP_GROUPS

ATTN_PATTERNS = ((128, 1), (512, 4), (2048, 16))
N_ATTN_GROUPS = len(ATTN_PATTERNS)
ATTN_HEADS = 8
HEAD_DIM = 64
ATTN_GROUP_WIDTH = ATTN_HEADS * HEAD_DIM
QKV_WIDTH = N_ATTN_GROUPS * ATTN_GROUP_WIDTH
ATTN_OUT_WIDTH = ATTN_GROUP_WIDTH
Q_BLOCK = 128

IN_WIDTH = 2 * GMLP_WIDTH + 3 * QKV_WIDTH + 2 * D_MODEL

D_FF = int(math.ceil(8 * D_MODEL / 3 / 256)) * 256

kernel_name = 'hybrid_gmlp_dilated_attn_block'


def rms_norm(x, g):
    x32 = x.astype(jnp.float32)
    y = x32 * lax.rsqrt(jnp.mean(x32 * x32, axis=-1, keepdims=True) + NORM_EPS)
    return (y * g.astype(jnp.float32)).astype(x.dtype)


def alibi_slopes():
    n = N_ATTN_GROUPS * ATTN_HEADS
    i = jnp.arange(1, n + 1, dtype=jnp.float32)
    return jnp.exp2(-8.0 * i / n).reshape(N_ATTN_GROUPS, ATTN_HEADS)


def spatial_gating(u, v, ln_g, ln_b, ws, bs):
    b, s, _ = u.shape
    u32 = jax.nn.gelu(u.astype(jnp.float32), approximate=False)
    v32 = jax.nn.gelu(v.astype(jnp.float32), approximate=False)
    mean = jnp.mean(v32, axis=-1, keepdims=True)
    var = jnp.mean(jnp.square(v32 - mean), axis=-1, keepdims=True)
    vn = (v32 - mean) * lax.rsqrt(var + NORM_EPS) * ln_g.astype(jnp.float32) + ln_b.astype(jnp.float32)
    vc = vn.reshape(b, s // CHUNK, CHUNK, GMLP_GROUPS, GMLP_GROUP_DIM)
    mixed = jnp.einsum('gts,bnsgc->bntgc', ws.astype(jnp.float32), vc)
    mixed = mixed + bs.astype(jnp.float32).T[:, :, None]
    return (u32 * mixed.reshape(b, s, GMLP_WIDTH)).astype(u.dtype)


def dilated_window_attention(q, k, v, window, dilation, slopes):
    b, s, h, dh = q.shape
    half = window // (2 * dilation)
    length = s // dilation
    n_blk = -(-length // Q_BLOCK)
    padded = n_blk * Q_BLOCK

    def to_sub(t):
        t = t.astype(jnp.float32).reshape(b, length, dilation, h, dh).transpose(0, 2, 1, 3, 4)
        return jnp.pad(t, ((0, 0), (0, 0), (0, padded - length), (0, 0), (0, 0)))

    def neighbours(t):
        tp = jnp.pad(t, ((0, 0), (0, 0), (Q_BLOCK, Q_BLOCK), (0, 0), (0, 0)))
        views = [tp[:, :, o * Q_BLOCK:o * Q_BLOCK + padded].reshape(b, dilation, n_blk, Q_BLOCK, h, dh)
                 for o in range(3)]
        return jnp.concatenate(views, axis=3)

    qb = to_sub(q).reshape(b, dilation, n_blk, Q_BLOCK, h, dh)
    kb = neighbours(to_sub(k))
    vb = neighbours(to_sub(v))

    qi = jnp.arange(Q_BLOCK)[:, None]
    kj = jnp.arange(3 * Q_BLOCK)[None, :]
    rel = kj - Q_BLOCK - qi
    key_idx = jnp.arange(n_blk)[:, None, None] * Q_BLOCK + (kj - Q_BLOCK)[None]
    valid = (jnp.abs(rel) <= half)[None] & (key_idx >= 0) & (key_idx < length)
    dist = (jnp.abs(rel) * dilation).astype(jnp.float32)

    scores = jnp.einsum('brnqhe,brnkhe->brnhqk', qb, kb) * (1.0 / math.sqrt(dh))
    scores = scores - slopes[:, None, None] * dist
    scores = jnp.where(valid[:, None], scores, NEG_INF)
    lse = jax.nn.logsumexp(scores, axis=-1)
    probs = jnp.exp(scores - lse[..., None])
    out = jnp.einsum('brnhqk,brnkhe->brnqhe', probs, vb)

    out = out.reshape(b, dilation, padded, h, dh)[:, :, :length]
    out = out.transpose(0, 2, 1, 3, 4).reshape(b, s, h, dh)
    lse = lse.transpose(0, 1, 2, 4, 3).reshape(b, dilation, padded, h)[:, :, :length]
    lse = lse.transpose(0, 2, 1, 3).reshape(b, s, h)
    return out, lse


def dilated_attention_mixer(q, k, v):
    b, s, _ = q.shape
    shp = (b, s, N_ATTN_GROUPS, ATTN_HEADS, HEAD_DIM)
    q, k, v = q.reshape(shp), k.reshape(shp), v.reshape(shp)
    slopes = alibi_slopes()
    outs, lses = [], []
    for g, (window, dil) in enumerate(ATTN_PATTERNS):
        o, l = dilated_window_attention(q[:, :, g], k[:, :, g], v[:, :, g], window, dil, slopes[g])
        outs.append(o)
        lses.append(l)
    outs = jnp.stack(outs, axis=0)
    weights = jax.nn.softmax(jnp.stack(lses, axis=0), axis=0)
    y = jnp.sum(weights[..., None] * outs, axis=0)
    return y.reshape(b, s, ATTN_OUT_WIDTH)


def swiglu(x, w_gate, w_up, w_down):
    return (jax.nn.silu(x @ w_gate) * (x @ w_up)) @ w_down


def setup_inputs(seed: int = 0) -> dict:
    key = jax.random.key(seed)
    ks = jax.random.split(key, 16)
    f32 = jnp.float32
    nrm = lambda k, shape, scale: jax.random.normal(k, shape, f32) * scale
    return {
        'x': jax.random.normal(ks[0], (BATCH, SEQ, D_MODEL), f32),
        'norm_mix_g': 1.0 + nrm(ks[1], (DEPTH, D_MODEL), 0.02),
        'w_in': nrm(ks[2], (DEPTH, D_MODEL, IN_WIDTH), D_MODEL ** -0.5),
        'gmlp_ln_g': 1.0 + nrm(ks[3], (DEPTH, GMLP_WIDTH), 0.02),
        'gmlp_ln_b': nrm(ks[4], (DEPTH, GMLP_WIDTH), 0.02),
        'gmlp_ws': nrm(ks[5], (DEPTH, GMLP_GROUPS, CHUNK, CHUNK), 0.5 * CHUNK ** -0.5),
        'gmlp_bs': 1.0 + nrm(ks[6], (DEPTH, GMLP_GROUPS, CHUNK), 0.1),
        'w_branch_gmlp': nrm(ks[7], (DEPTH, GMLP_WIDTH, D_MODEL), GMLP_WIDTH ** -0.5),
        'w_branch_attn': nrm(ks[8], (DEPTH, ATTN_OUT_WIDTH, D_MODEL), ATTN_OUT_WIDTH ** -0.5),
        'w_out': nrm(ks[9], (DEPTH, D_MODEL, D_MODEL), D_MODEL ** -0.5),
        'norm_ffn_g': 1.0 + nrm(ks[10], (DEPTH, D_MODEL), 0.02),
        'w_ffn_gate': nrm(ks[11], (DEPTH, D_MODEL, D_FF), D_MODEL ** -0.5),
        'w_ffn_up': nrm(ks[12], (DEPTH, D_MODEL, D_FF), D_MODEL ** -0.5),
        'w_ffn_down': nrm(ks[13], (DEPTH, D_FF, D_MODEL), D_FF ** -0.5),
        'norm_final_g': 1.0 + nrm(ks[14], (D_MODEL,), 0.02),
    }


def reference(x, norm_mix_g, w_in, gmlp_ln_g, gmlp_ln_b, gmlp_ws, gmlp_bs, w_branch_gmlp,
              w_branch_attn, w_out, norm_ffn_g, w_ffn_gate, w_ffn_up, w_ffn_down, norm_final_g):
    widths = [GMLP_WIDTH, GMLP_WIDTH, QKV_WIDTH, QKV_WIDTH, QKV_WIDTH, D_MODEL, D_MODEL]
    splits = [int(c) for c in np.cumsum(widths)[:-1]]
    h = x
    for l in range(DEPTH):
        xn = rms_norm(h, norm_mix_g[l])
        proj = xn @ w_in[l]
        u, vg, q, k, va, gate_a, gate_b = jnp.split(proj, splits, axis=-1)
        y_gmlp = spatial_gating(u, vg, gmlp_ln_g[l], gmlp_ln_b[l], gmlp_ws[l], gmlp_bs[l])
        y_attn = dilated_attention_mixer(q, k, va).astype(h.dtype)
        merged = (jax.nn.sigmoid(gate_a) * (y_gmlp @ w_branch_gmlp[l])
                  + jax.nn.sigmoid(gate_b) * (y_attn @ w_branch_attn[l]))
        h = h + merged @ w_out[l]
        h = h + swiglu(rms_norm(h, norm_ffn_g[l]), w_ffn_gate[l], w_ffn_up[l], w_ffn_down[l])
    return rms_norm(h, norm_final_g)
```

```python
import numpy as np
import ml_dtypes
import concourse.bass as bass
import concourse.mybir as mybir
from concourse.bass_utils import run_bass_kernel_spmd

F32 = mybir.dt.float32
BF16 = mybir.dt.bfloat16
AF = mybir.ActivationFunctionType
ALU = mybir.AluOpType

D = 1024
SEQ = 8192
BATCH = 4
NOWN = 4096
NTOK = 5120
T = 512
PAD = 1024
NROW = PAD + NTOK
IN_W = 8704
COL_U, COL_V, COL_Q, COL_K, COL_VA, COL_GA, COL_GB = 0, 1024, 2048, 3584, 5120, 6656, 7680
DFF = 2816
NFC = 22
DIL = (1, 4, 16)
EPS = 1e-6
NEG = -30000.0
BLK = 256


class Trk:
    def __init__(self, nc):
        self.nc = nc
        self.eng = dict(pe=nc.tensor, act=nc.scalar, dve=nc.vector, pool=nc.gpsimd, sp=nc.sync)
        self.sem = {}
        self.cnt = {}
        self.waited = {e: {} for e in self.eng}
        for e in ("pe", "act", "dve", "pool"):
            self.sem[e] = nc.alloc_semaphore("s_" + e)
            self.cnt[e] = 0
        self.st = {}
        self.nwait = 0

    def stream(self, name):
        k = "d:" + name
        if k not in self.sem:
            self.sem[k] = self.nc.alloc_semaphore("sd_" + name)
            self.cnt[k] = 0
        return k

    def _state(self, key):
        s = self.st.get(key)
        if s is None:
            s = [None, {}]
            self.st[key] = s
        return s

    def _gather(self, R, W):
        need = {}

        def add(ev):
            if ev is None:
                return
            k, v = ev
            if v > need.get(k, 0):
                need[k] = v

        for key in R:
            add(self._state(key)[0])
        for key in W:
            s = self._state(key)
            add(s[0])
            for k, v in s[1].items():
                add((k, v))
        return need

    def _wait(self, eng, need):
        for k, v in need.items():
            if eng == "pe" and k == "pe":
                continue
            if self.waited[eng].get(k, 0) >= v:
                continue
            self.eng[eng].wait_ge(self.sem[k], v)
            self.waited[eng][k] = v
            self.nwait += 1

    def _record(self, R, W, ev):
        k, v = ev
        for key in R:
            s = self._state(key)
            if v > s[1].get(k, 0):
                s[1][k] = v
        for key in W:
            s = self._state(key)
            s[0] = ev
            s[1] = {}

    def op(self, eng, fn, R=(), W=(), sig=True):
        self._wait(eng, self._gather(R, W))
        ins = fn(self.eng[eng])
        if sig:
            self.cnt[eng] += 1
            ins.then_inc(self.sem[eng], 1)
            ev = (eng, self.cnt[eng])
        else:
            ev = (eng, self.cnt[eng] + 1)
        self._record(R, W, ev)

    def dma(self, q, out, in_, R=(), W=(), stream="ld"):
        k = self.stream(stream)
        self._wait(q, self._gather(R, W))
        self.cnt[k] += 16
        self.eng[q].dma_start(out=out, in_=in_).then_inc(self.sem[k], 16)
        self._record(R, W, (k, self.cnt[k]))


def sl(start, n, step=1):
    return slice(start, start + step * (n - 1) + 1, step)


class V:
    def __init__(self, ap, keys, stream=None):
        self.ap = ap
        self.keys = keys
        self.stream = stream


def build_program():
    nc = bass.Bass("TRN2", target_bir_lowering=False)
    tk = Trk(nc)

    def din(name, shape, dt=F32):
        return nc.dram_tensor(name, list(shape), dt, kind="ExternalInput")

    x_d = din("x", [NTOK, D])
    w_in_d = din("w_in", [D, IN_W])
    w_a_d = din("w_a", [D, D])
    w_b_d = din("w_b", [512, D])
    w_out_d = din("w_out", [D, D])
    w_gate_d = din("w_gate", [D, DFF])
    w_up_d = din("w_up", [D, DFF])
    w_down_d = din("w_down", [DFF, D])
    gmix_d = din("gmix", [1, D])
    gffn_d = din("gffn", [1, D])
    gfin_d = din("gfin", [1, D])
    lng_d = din("lng", [1, D])
    lnb_d = din("lnb", [1, D])
    bs_d = din("bs", [1, D])
    wst_d = din("wst", [128, 1024])
    bt01_d = din("bt01", [128, 2 * 8 * 256])
    bt2_d = din("bt2", [128, 8 * 64])
    idn_d = din("idn", [128, 128])
    y_d = nc.dram_tensor("y", [NOWN, D], F32, kind="ExternalOutput")

    wb_in = nc.dram_tensor("wb_in", [D, IN_W], BF16, kind="Internal")
    wb_a = nc.dram_tensor("wb_a", [D, D], BF16, kind="Internal")
    wb_b = nc.dram_tensor("wb_b", [512, D], BF16, kind="Internal")
    wb_out = nc.dram_tensor("wb_out", [D, D], BF16, kind="Internal")
    wb_gate = nc.dram_tensor("wb_gate", [D, DFF], BF16, kind="Internal")
    wb_up = nc.dram_tensor("wb_up", [D, DFF], BF16, kind="Internal")
    wb_down = nc.dram_tensor("wb_down", [DFF, D], BF16, kind="Internal")
    kscr = nc.dram_tensor("kscr", [3, 4, 128, NROW], BF16, kind="Internal")
    vscr = nc.dram_tensor("vscr", [3, NROW, 520], BF16, kind="Internal")

    KB = 1024
    SH = {}
    off = 0

    def shared(name, nbytes):
        nonlocal off
        SH[name] = off
        off += nbytes

    shared("WS", 3 * 8 * KB)
    shared("XH", 16 * KB)
    for nm in ("GMIX", "GFFN", "GFIN", "LG", "LB", "BSB"):
        shared(nm, 4 * KB)
    shared("BT01", 16 * KB)
    shared("BT2", 2 * KB)
    shared("WST", 2 * KB)
    shared("IDN", 512)
    shared("IDNB", 256)
    shared("ONE", 256)
    shared("SML", 1024)
    AR = off
    ARENA_BYTES = int(121.5 * KB)
    TOTAL = AR + ARENA_BYTES
    assert TOTAL <= 212860, TOTAL
    sb = nc.alloc_sbuf_tensor("sb", [128, TOTAL // 2], BF16)

    def view(off_b, nbytes, dt, keys=None, name=None):
        ap = sb[:, off_b // 2:(off_b + nbytes) // 2]
        if dt == F32:
            ap = ap.bitcast(F32)
        if keys is None:
            if name is not None:
                keys = [("n", name)]
            else:
                keys = [("b", i) for i in range(off_b // BLK, (off_b + nbytes - 1) // BLK + 1)]
        return V(ap, keys)

    def av(off_kb, nbytes, dt):
        return view(AR + int(off_kb * KB), nbytes, dt)

    WS = [view(SH["WS"] + i * 8 * KB, 8 * KB, BF16, name="ws%d" % i) for i in range(3)]
    XH = [view(SH["XH"] + tb * 4 * KB, 4 * KB, F32, name="xh%d" % tb) for tb in range(4)]
    XH_all = view(SH["XH"], 16 * KB, F32, keys=[("n", "xh%d" % tb) for tb in range(4)])
    GMIX = view(SH["GMIX"], 4 * KB, F32, name="gmix")
    GFFN = view(SH["GFFN"], 4 * KB, F32, name="gffn")
    GFIN = view(SH["GFIN"], 4 * KB, F32, name="gfin")
    LG = view(SH["LG"], 4 * KB, F32, name="lg")
    LB = view(SH["LB"], 4 * KB, F32, name="lb")
    BSB = view(SH["BSB"], 4 * KB, F32, name="bsb")
    BT01 = view(SH["BT01"], 16 * KB, F32, name="bt01")
    BT2 = view(SH["BT2"], 2 * KB, F32, name="bt2")
    WST = view(SH["WST"], 2 * KB, BF16, name="wst")
    IDN = view(SH["IDN"], 512, F32, name="idn")
    IDNB = view(SH["IDNB"], 256, BF16, name="idnb")
    ONE = view(SH["ONE"], 256, F32, name="one")
    SML = view(SH["SML"], 1024, F32, name="sml")
    sml_next = [0]

    def small(ncols, name):
        c0 = sml_next[0]
        sml_next[0] += ncols
        assert sml_next[0] <= 256
        return V(SML.ap[:, c0:c0 + ncols], [("n", "sml_" + name)])

    XS = [av(0, 2 * KB, BF16), av(2, 2 * KB, BF16)]
    XSF = av(4, 4 * KB, F32)
    JNK = av(8, 2 * KB, BF16)
    XNT = [av(10 + kc, KB, BF16) for kc in range(8)]
    XNT3 = av(10, 8 * KB, BF16)
    M1 = [av(18 + 2 * oc, 2 * KB, F32) for oc in range(8)]
    GU = [av(34 + oc, KB, BF16) for oc in range(8)]
    QT = [[[av((34 if s == 0 else 0) + 2 * c + v, KB, BF16) for v in range(2)] for c in range(4)] for s in range(2)]
    KST = av(34, 4 * KB, BF16)
    VSTG = [av(42 + i * 1.25, 1040, BF16) for i in range(4)]
    VN = [av(42 + 2 * tb, 2 * KB, BF16) for tb in range(4)]
    ACC = [av(42 + 2 * h, 2 * KB, F32) for h in range(8)]
    VST = [av(50, 4 * KB, F32), av(54, 4 * KB, F32)]
    SA = [av(58 + oc, KB, BF16) for oc in range(8)]
    SBF = [av(58, 2 * KB, F32), av(60, 2 * KB, F32)]
    PT = [av(62 + i, KB, BF16) for i in range(3)]
    KW = av(66, 20 * KB, BF16)
    YAT = [av(66 + h, KB, BF16) for h in range(8)]
    SBG = [av(74 + oc, KB, BF16) for oc in range(8)]
    AT = [av(66 + fc, KB, BF16) for fc in range(NFC)]
    VA = [view(AR + 86 * KB + i * 1040, 1040, BF16) for i in range(8)]
    VB = [view(AR + 86 * KB + 8320 + i * 1040, 1040, BF16) for i in range(8)]
    MT = [av(0 + oc, KB, BF16) for oc in range(8)]
    PT2 = [av(118.5 + i, KB, BF16) for i in range(3)]
    T12 = [av(102.5 + 2 * i, 2 * KB, F32) for i in range(3)]
    SG = [T12[0], T12[1]]
    KW1 = av(108.5, 8 * KB, BF16)
    XP = [av(18 + 4 * tb, 4 * KB, F32) for tb in range(4)]
    XP_all = av(18, 16 * KB, F32)
    VA_all = view(AR + 86 * KB, 8320, BF16)
    VB_all = view(AR + 86 * KB + 8320, 8320, BF16)
    ZERO = av(0, 8320, BF16)

    ps = nc.alloc_psum_tensor("ps", [128, 8, 512], F32)
    bank_i = [0]

    def nb():
        b = bank_i[0] % 8
        bank_i[0] += 1
        return b

    def PS(b):
        return ps[:, b, :]

    def PK(b):
        return [("ps", b)]

    def v3(v, a):
        return v.ap.rearrange("p (a b) -> p a b", a=a)

    def ld(out_v, in_ap, stream="c", extraR=()):
        tk.dma("sp", out_v.ap, in_ap, R=list(extraR), W=out_v.keys, stream=stream)

    consts = [GMIX, GFFN, GFIN, LG, LB, BSB, BT01, BT2, IDN, XSF]
    for vv, dd in ((GMIX, gmix_d), (GFFN, gffn_d), (GFIN, gfin_d), (LG, lng_d), (LB, lnb_d), (BSB, bs_d)):
        ld(vv, dd[0:1, :].partition_broadcast(128))
    ld(BT01, bt01_d[:, :])
    ld(BT2, bt2_d[:, :])
    ld(IDN, idn_d[:, :])
    ld(XSF, wst_d[:, :])
    kc_ = tk.stream("c")
    for vv in consts:
        for key in vv.keys:
            tk._state(key)[0] = (kc_, tk.cnt[kc_])
    tk.op("dve", lambda e: e.tensor_copy(out=WST.ap, in_=XSF.ap), R=XSF.keys, W=WST.keys)
    tk.op("dve", lambda e: e.tensor_copy(out=IDNB.ap, in_=IDN.ap), R=IDN.keys, W=IDNB.keys)
    tk.op("dve", lambda e: e.memset(ONE.ap, 1.0), W=ONE.keys)
    MHALF = small(1, "mhalf")
    SEO = [small(1, "seo0"), small(1, "seo1")]
    tk.op("pool", lambda e: e.memset(SEO[0].ap[0:64, :], 0.125), W=SEO[0].keys)
    tk.op("pool", lambda e: e.memset(SEO[0].ap[64:128, :], 0.0), W=SEO[0].keys)
    tk.op("pool", lambda e: e.memset(SEO[1].ap[0:64, :], 0.0), W=SEO[1].keys)
    tk.op("pool", lambda e: e.memset(SEO[1].ap[64:128, :], 0.125), W=SEO[1].keys)
    tk.op("pool", lambda e: e.memset(MHALF.ap, -0.5), W=MHALF.keys)

    tk.op("pool", lambda e: e.memset(ZERO.ap, 0.0), W=ZERO.keys)
    KV_KEYS = [("dr", "kv", i) for i in range(8)]
    for g in range(3):
        tk.dma("sp", vscr[g, 0:PAD, :].rearrange("(p a) c -> p (a c)", a=8), ZERO.ap,
               R=ZERO.keys, W=KV_KEYS[0:1], stream="kvz")
        tk.dma("sp", kscr[g, :, :, 0:PAD].rearrange("c p t -> p c t"),
               ZERO.ap[:, 0:4096].rearrange("p (c t) -> p c t", c=4),
               R=ZERO.keys, W=KV_KEYS[0:1], stream="kvz")
    kz_ = tk.stream("kvz")
    for key in ZERO.keys:
        tk._state(key)[1][kz_] = tk.cnt[kz_]
    for pt_ in PT2:
        tk.op("pool", lambda e: e.memset(pt_.ap, 0.0), W=pt_.keys)
    tk.op("pool", lambda e: e.memset(VB_all.ap, 0.0), W=VB_all.keys)
    for i in range(4):
        tk.op("pool", lambda e, i=i: e.memset(VSTG[i].ap, 1.0), W=VSTG[i].keys)

    conv_jobs = []

    def convert(src, dst, name, rows, c0, c1):
        c = c0
        while c < c1:
            w = min(2048, c1 - c)
            keys = [("w", name, cg) for cg in range(c // 512, (c + w - 1) // 512 + 1)]
            for r in range(0, rows, 128):
                conv_jobs.append((dst[r:r + 128, c:c + w], src[r:r + 128, c:c + w], keys, "cv_%s_%d" % (name, c)))
            c += w

    def issue_conv(n, ffn=True):
        while n > 0 and conv_jobs:
            st0 = conv_jobs[0][3]
            if (not ffn) and st0.split("_")[1] in ("gate", "up", "down"):
                return
            while conv_jobs and conv_jobs[0][3] == st0:
                o, i, keys, st = conv_jobs.pop(0)
                tk.dma("pool", o, i, W=keys, stream=st)
                n -= 1
        return
        for _ in range(0):
            o, i, keys, st = conv_jobs.pop(0)
            tk.dma("pool", o, i, W=keys, stream=st)

    convert(w_in_d, wb_in, "in", D, COL_K, COL_GA)
    convert(w_in_d, wb_in, "in", D, 0, COL_K)
    convert(w_in_d, wb_in, "in", D, COL_GA, IN_W)
    convert(w_a_d, wb_a, "a", D, 0, D)
    convert(w_b_d, wb_b, "b", 512, 0, D)
    convert(w_out_d, wb_out, "out", D, 0, D)
    convert(w_gate_d, wb_gate, "gate", D, 0, 2048)
    convert(w_up_d, wb_up, "up", D, 0, 2048)
    convert(w_gate_d, wb_gate, "gate", D, 2048, DFF)
    convert(w_up_d, wb_up, "up", D, 2048, DFF)
    convert(w_down_d, wb_down, "down", DFF, 0, D)
    issue_conv(16)

    ws_i = [0]

    def load_w(dram, name, c0, ncols, krows=D):
        s = ws_i[0] % 3
        ws_i[0] += 1
        nkc = krows // 128
        dst = WS[s].ap[:, 0:nkc * ncols].rearrange("p (k c) -> p k c", k=nkc)
        src = dram[0:krows, c0:c0 + ncols].rearrange("(k p) c -> p k c", p=128)
        keys = [("w", name, cg) for cg in range(c0 // 512, (c0 + ncols - 1) // 512 + 1)]
        tk.dma("sp", dst, src, R=keys, W=WS[s].keys, stream="w%d" % s)
        return WS[s], dst

    SS = [small(1, "ss%d" % i) for i in range(4)]
    TA = [small(1, "ta%d" % i) for i in range(4)]
    RS = [small(1, "rs%d" % i) for i in range(4)]
    TB = [small(1, "tb%d" % i) for i in range(4)]
    use_pool_pow = [False]

    def rstd_of(src_v, i, n_inv):
        tk.op("act", lambda e: e.activation(out=JNK.ap, in_=src_v.ap, func=AF.Square, accum_out=SS[i].ap),
              R=src_v.keys, W=JNK.keys + SS[i].keys)
        tk.op("dve", lambda e: e.tensor_scalar(out=TA[i].ap, in0=SS[i].ap, scalar1=n_inv, scalar2=EPS,
                                               op0=ALU.mult, op1=ALU.add), R=SS[i].keys, W=TA[i].keys)
        if use_pool_pow[0]:
            tk.op("pool", lambda e: e.tensor_tensor(out=RS[i].ap, in0=TA[i].ap, in1=MHALF.ap, op=ALU.pow),
                  R=TA[i].keys + MHALF.keys, W=RS[i].keys)
        else:
            tk.op("act", lambda e: e.activation(out=TB[i].ap, in_=TA[i].ap, func=AF.Sqrt), R=TA[i].keys, W=TB[i].keys)
            tk.op("dve", lambda e: e.reciprocal(out=RS[i].ap, in_=TB[i].ap), R=TB[i].keys, W=RS[i].keys)

    def norm_scale(gain_v, SRC, tb, xs):
        tk.op("dve", lambda e: e.scalar_tensor_tensor(out=xs.ap, in0=SRC[tb].ap, scalar=RS[tb].ap, in1=gain_v.ap,
                                                      op0=ALU.mult, op1=ALU.mult),
              R=SRC[tb].keys + RS[tb].keys + gain_v.keys, W=xs.keys)

    def norm_xpose(tb, xs, XNT3_, XNT_):
        for hf in range(2):
            b = nb()
            for q in range(4):
                kc = hf * 4 + q
                tk.op("pe", lambda e: e.transpose(out=ps[:, b, :].bitcast(BF16)[:, q * 128:(q + 1) * 128],
                                                  in_=xs.ap[:, kc * 128:(kc + 1) * 128], identity=IDNB.ap),
                      R=xs.keys + IDNB.keys, W=PK(b), sig=(q == 3))
            outap = v3(XNT3_, 8)[:, hf * 4:hf * 4 + 4, tb * 128:(tb + 1) * 128]
            inap = ps[:, b, :].bitcast(BF16)[:, 0:512].rearrange("p (a b) -> p a b", a=4)
            wk = []
            for kc in range(hf * 4, hf * 4 + 4):
                wk += XNT_[kc].keys
            tk.op("act", lambda e: e.activation(out=outap, in_=inap, func=AF.Copy), R=PK(b), W=wk)

    def norm_T(gain_v, SRC=None, tbs=(0, 1, 2, 3), DST=None):
        if SRC is None:
            SRC = XH
        XNT3_, XNT_ = (XNT3, XNT) if DST is None else DST
        for tb in tbs:
            rstd_of(SRC[tb], tb, 1.0 / D)
        for tb in tbs:
            xs = XS[tb % 2]
            norm_scale(gain_v, SRC, tb, xs)
            norm_xpose(tb, xs, XNT3_, XNT_)

    def load_x(j, dst=None, stream="x"):
        if dst is None:
            dst = XH_all
        tk.dma("sp", v3(dst, 4), x_d[j * T:(j + 1) * T, :].rearrange("(tb p) f -> p tb f", p=128),
               W=dst.keys, stream=stream)

    LA = 2
    jobs = []
    for j in range(NTOK // T):
        for g in ((0, 1, 2) if j < 9 else (2,)):
            jobs.append((j, g, "K"))
            jobs.append((j, g, "V"))
    wq = {}

    RES = {}
    res_bufs = [WS[0], WS[1], WS[2], av(18, 8 * KB, BF16), av(26, 8 * KB, BF16), av(102.5, 8 * KB, BF16)]

    def wload(idx):
        if idx < len(jobs) and idx not in wq:
            j_, g_, kind_ = jobs[idx]
            if (g_, kind_) not in RES:
                buf = res_bufs[len(RES)]
                c0_ = (COL_K if kind_ == "K" else COL_VA) + 512 * g_
                dst_ = buf.ap[:, 0:4096].rearrange("p (k c) -> p k c", k=8)
                src_ = wb_in[0:D, c0_:c0_ + 512].rearrange("(k p) c -> p k c", p=128)
                tk.dma("sp", dst_, src_, W=buf.keys + [("w", "in", c0_ // 512)], stream="res%d" % len(RES))
                RES[(g_, kind_)] = (buf, dst_)
            wq[idx] = RES[(g_, kind_)]

    XNTB = [av(50 + kc, KB, BF16) for kc in range(8)]
    XNTB3 = av(50, 8 * KB, BF16)
    XHB = [av(58 + 4 * tb, 4 * KB, F32) for tb in range(4)]
    XHB_all = av(58, 16 * KB, F32)
    xbuf = [(XH, XH_all, "x"), (XHB, XHB_all, "xb")]
    nbuf = [(XNT3, XNT), (XNTB3, XNTB)]
    NT1 = NTOK // T
    XS4 = [av(74 + 2 * i, 2 * KB, BF16) for i in range(4)]
    KSTS = [KST, av(82, 4 * KB, BF16)]
    kst_i = [0]
    load_x(0, xbuf[0][1], xbuf[0][2])
    wload(0)
    wload(1)
    norm_T(GMIX, xbuf[0][0], DST=nbuf[0])
    cur_tile = -1
    job_in_tile = 0
    for idx, (j, g, kind) in enumerate(jobs):
        if j != cur_tile:
            cur_tile = j
            job_in_tile = 0
            if j + 1 < NT1:
                load_x(j + 1, xbuf[(j + 1) % 2][1], xbuf[(j + 1) % 2][2])
            issue_conv(8, ffn=False)
        XNTc = nbuf[j % 2][1]
        if j + 1 < NT1:
            SRCn = xbuf[(j + 1) % 2][0]
            if job_in_tile == 1:
                for tb in range(4):
                    rstd_of(SRCn[tb], tb, 1.0 / D)
                for tb in range(4):
                    norm_scale(GMIX, SRCn, tb, XS4[tb])
            if job_in_tile == 4:
                for tb in range(4):
                    norm_xpose(tb, XS4[tb], nbuf[(j + 1) % 2][0], nbuf[(j + 1) % 2][1])
        job_in_tile += 1
        wsv, wap = wq.pop(idx)
        wload(idx + 1)
        wload(idx + 2)
        if kind == "K":
            KSTc = KSTS[kst_i[0] % 2]
            kst_s = "kst%d" % (kst_i[0] % 2)
            kst_i[0] += 1
            for c in range(4):
                b = nb()
                for kc in range(8):
                    tk.op("pe", lambda e: e.matmul(PS(b), wap[:, kc, c * 128:(c + 1) * 128], XNTc[kc].ap,
                                                   start=(kc == 0), stop=(kc == 7)),
                          R=wsv.keys + XNTc[kc].keys, W=PK(b), sig=(kc == 7))
                tk.op("act", lambda e: e.activation(out=KSTc.ap[:, c * 512:(c + 1) * 512], in_=PS(b), func=AF.Copy),
                      R=PK(b), W=KSTc.keys)
            tk.dma("act", kscr[g, :, :, PAD + j * T:PAD + (j + 1) * T].rearrange("c p t -> p c t"),
                   v3(KSTc, 4), R=KSTc.keys, W=[("dr", "kv", 6 + (kst_i[0] - 1) % 2)], stream=kst_s)
        else:
            for tb in range(4):
                b = nb()
                for kc in range(8):
                    tk.op("pe", lambda e: e.matmul(PS(b), XNTc[kc].ap[:, tb * 128:(tb + 1) * 128], wap[:, kc, :],
                                                   start=(kc == 0), stop=(kc == 7)),
                          R=wsv.keys + XNTc[kc].keys, W=PK(b), sig=(kc == 7))
                vs = VSTG[tb]
                tk.op("dve", lambda e: e.tensor_copy(out=v3(vs, 8)[:, :, 0:64],
                                                     in_=ps[:, b, :].rearrange("p (h e) -> p h e", h=8)),
                      R=PK(b), W=vs.keys)
                r0 = PAD + j * T + tb * 128
                tk.dma("act", vscr[g, r0:r0 + 128, :], vs.ap, R=vs.keys, W=KV_KEYS[2 + tb:3 + tb], stream="vst%d" % tb)

    issue_conv(len(conv_jobs), ffn=False)
    issue_conv(len(conv_jobs))
    ST6 = [small(12, "st6_%d" % i) for i in range(2)]
    MV = [small(2, "mv%d" % i) for i in range(2)]
    pvb_i = [0]
    scb_i = [0]
    pt_i = [0]
    sbf_i = [0]

    def fm_proj(dram, name, c0, rhs_list, nk, epilogue, krows=D, k64=False):
        for half in range(2):
            wsv, wap = load_w(dram, name, c0 + 512 * half, 512, krows=krows)
            for q in range(4):
                oc = half * 4 + q
                b = nb()
                for kc in range(nk):
                    rv = rhs_list[kc]
                    if k64:
                        lhs = wap[0:64, kc, q * 128:(q + 1) * 128]
                        rhs = rv.ap[0:64, :]
                    else:
                        lhs = wap[:, kc, q * 128:(q + 1) * 128]
                        rhs = rv.ap
                    tk.op("pe", lambda e: e.matmul(PS(b), lhs, rhs, start=(kc == 0), stop=(kc == nk - 1)),
                          R=wsv.keys + rv.keys, W=PK(b), sig=(kc == nk - 1))
                epilogue(oc, b)

    for j in range(NOWN // T):
        if j == 0:
            load_x(0, XP_all, "xp")
            norm_T(GMIX, XP)

        wv = [load_w(wb_in, "in", COL_V + 512 * hf, 512) for hf in range(2)]
        for tb in range(4):
            vst = VST[tb % 2]
            st6 = ST6[tb % 2]
            mv = MV[tb % 2]
            for hf in range(2):
                b = nb()
                wsv, wap = wv[hf]
                for kc in range(8):
                    tk.op("pe", lambda e: e.matmul(PS(b), XNT[kc].ap[:, tb * 128:(tb + 1) * 128], wap[:, kc, :],
                                                   start=(kc == 0), stop=(kc == 7)),
                          R=wsv.keys + XNT[kc].keys, W=PK(b), sig=(kc == 7))
                tk.op("act", lambda e: e.activation(out=vst.ap[:, hf * 512:(hf + 1) * 512], in_=PS(b), func=AF.Gelu),
                      R=PK(b), W=vst.keys)
                tk.op("dve", lambda e: e.bn_stats(out=st6.ap[:, hf * 6:(hf + 1) * 6], in_=vst.ap[:, hf * 512:(hf + 1) * 512]),
                      R=vst.keys, W=st6.keys)
            tk.op("dve", lambda e: e.bn_aggr(out=mv.ap, in_=st6.ap), R=st6.keys, W=mv.keys)
            tk.op("dve", lambda e: e.tensor_scalar(out=TA[tb].ap, in0=mv.ap[:, 1:2], scalar1=EPS, scalar2=None, op0=ALU.add),
                  R=mv.keys, W=TA[tb].keys)
            pe_ = "dve" if j == 0 else "pool"
            if j == 0:
                tk.op("act", lambda e: e.activation(out=TB[tb].ap, in_=TA[tb].ap, func=AF.Sqrt), R=TA[tb].keys, W=TB[tb].keys)
                tk.op("dve", lambda e: e.reciprocal(out=RS[tb].ap, in_=TB[tb].ap), R=TB[tb].keys, W=RS[tb].keys)
            else:
                tk.op("pool", lambda e: e.tensor_tensor(out=RS[tb].ap, in0=TA[tb].ap, in1=MHALF.ap, op=ALU.pow),
                      R=TA[tb].keys + MHALF.keys, W=RS[tb].keys)
            tk.op("dve", lambda e: e.tensor_scalar(out=vst.ap, in0=vst.ap, scalar1=mv.ap[:, 0:1], scalar2=RS[tb].ap,
                                                   op0=ALU.subtract, op1=ALU.mult),
                  R=vst.keys + mv.keys + RS[tb].keys, W=vst.keys)
            tk.op(pe_, lambda e: e.tensor_tensor(out=vst.ap, in0=vst.ap, in1=LG.ap, op=ALU.mult),
                  R=vst.keys + LG.keys, W=vst.keys)
            tk.op(pe_, lambda e: e.tensor_tensor(out=VN[tb].ap, in0=vst.ap, in1=LB.ap, op=ALU.add),
                  R=vst.keys + LB.keys, W=VN[tb].keys)

        def ep_u(oc, b):
            tk.op("act", lambda e: e.activation(out=GU[oc].ap, in_=PS(b), func=AF.Gelu), R=PK(b), W=GU[oc].keys)
        fm_proj(wb_in, "in", COL_U, XNT, 8, ep_u)

        def spatial_group(g):
            b = nb()
            for tb in range(4):
                tk.op("pe", lambda e: e.matmul(ps[:, b, tb * 128:(tb + 1) * 128], VN[tb].ap[:, g * 128:(g + 1) * 128],
                                               WST.ap[:, g * 128:(g + 1) * 128], start=True, stop=True),
                      R=VN[tb].keys + WST.keys, W=PK(b), sig=(tb == 3))
            t1 = T12[g % 2]
            bsap = BSB.ap[:, g * 128:(g + 1) * 128].unsqueeze(1).broadcast_to([128, 4, 128])
            tk.op("dve", lambda e: e.tensor_tensor(out=v3(t1, 4), in0=ps[:, b, :].rearrange("p (a b) -> p a b", a=4),
                                                   in1=bsap, op=ALU.add),
                  R=PK(b) + BSB.keys, W=t1.keys)
            tk.op("dve" if j == 0 else "pool", lambda e: e.tensor_tensor(out=GU[g].ap, in0=t1.ap, in1=GU[g].ap, op=ALU.mult),
                  R=t1.keys + GU[g].keys, W=GU[g].keys)

        def ep_ga(oc, b):
            tk.op("act", lambda e: e.activation(out=SA[oc].ap, in_=PS(b), func=AF.Sigmoid), R=PK(b), W=SA[oc].keys)
            spatial_group(oc)
        fm_proj(wb_in, "in", COL_GA, XNT, 8, ep_ga)

        def ep_m1(oc, b):
            tk.op("dve", lambda e: e.tensor_tensor(out=M1[oc].ap, in0=PS(b), in1=SA[oc].ap, op=ALU.mult),
                  R=PK(b) + SA[oc].keys, W=M1[oc].keys)
        fm_proj(wb_a, "a", 0, GU, 8, ep_m1)

        for g in range(3):
            d = DIL[g]
            qt = QT[g % 2]
            wsv, wap = load_w(wb_in, "in", COL_Q + 512 * g, 512)
            for c in range(4):
                b = nb()
                for kc in range(8):
                    tk.op("pe", lambda e: e.matmul(PS(b), wap[:, kc, c * 128:(c + 1) * 128], XNT[kc].ap,
                                                   start=(kc == 0), stop=(kc == 7)),
                          R=wsv.keys + XNT[kc].keys, W=PK(b), sig=(kc == 7))
                for v_ in range(2):
                    tk.op("dve", lambda e: e.tensor_scalar(out=qt[c][v_].ap, in0=PS(b), scalar1=SEO[v_].ap, scalar2=None,
                                                           op0=ALU.mult),
                          R=PK(b) + SEO[v_].keys, W=qt[c][v_].keys)
            Wg = T + 128 * d
            wstart = PAD + j * T - 64 * d
            KWg = KW1 if g == 1 else KW
            kw3 = KWg.ap[:, 0:4 * Wg].rearrange("p (c t) -> p c t", c=4)
            tk.dma("sp", kw3, kscr[g, :, :, wstart:wstart + Wg].rearrange("c p t -> p c t"),
                   R=KV_KEYS, W=KWg.keys, stream=("kw1" if g == 1 else "kw"))

            base = PAD + j * T

            def mk_units(pi):
                units = []
                if g == 0:
                    tk.dma("sp", VA_all.ap[:, 0:5 * 520].rearrange("p (m c) -> p m c", m=5),
                           vscr[g, base - 64:base - 64 + 640, :].rearrange("(m p) c -> p m c", p=128),
                           R=KV_KEYS, W=VA_all.keys, stream="va_a")
                    for qb in range(4):
                        units.append(dict(q0=128 * qb, qs=1, nq=128,
                                          tiles=[(128 * qb, 1, 128, VA[qb]), (128 * qb + 128, 1, 128, VA[qb + 1])]))
                elif g == 1:
                    ka = []
                    kb = []
                    for i_ in range(4):
                        ka += VA[i_].keys
                        kb += VA[4 + i_].keys
                    tk.dma("sp", VA_all.ap[:, 0:4 * 520],
                           vscr[g, base - 256:base - 256 + 512, :].rearrange("(p r) c -> p (r c)", r=4),
                           R=KV_KEYS, W=ka, stream="va_a")
                    tk.dma("sp", VA_all.ap[:, 4 * 520:8 * 520],
                           vscr[g, base + 256:base + 256 + 512, :].rearrange("(p r) c -> p (r c)", r=4),
                           R=KV_KEYS, W=kb, stream="va_b")
                    for r in range(4):
                        units.append(dict(q0=r, qs=4, nq=128, tiles=[(r, 4, 128, VA[r]), (r + 512, 4, 128, VA[4 + r])]))
                else:
                    tk.dma("sp", VA_all.ap.rearrange("p (r c) -> p r c", r=8),
                           vscr[g, base - 1024:base - 1024 + 2048, :].rearrange("(p r) c -> p r c", r=16)[:, 8 * pi:8 * pi + 8, :],
                           R=KV_KEYS, W=VA_all.keys, stream="va_a")
                    tk.dma("sp", VB_all.ap[0:32, :].rearrange("p (r c) -> p r c", r=8),
                           vscr[g, base + 1024:base + 1024 + 512, :].rearrange("(p r) c -> p r c", r=16)[:, 8 * pi:8 * pi + 8, :],
                           R=KV_KEYS, W=VB_all.keys, stream="vb")
                    for rr in range(8):
                        r = 8 * pi + rr
                        units.append(dict(q0=r, qs=16, nq=32, tiles=[(r, 16, 128, VA[rr]), (r + 2048, 16, 32, VB[rr])]))
                return units

            for pi in range(2 if g == 2 else 1):
                units = mk_units(pi)
                blist = []
                for h in range(8):
                    bl = [units[0:2], units[2:4]] if g < 2 else [units]
                    for bi, bu in enumerate(bl):
                        blist.append((h, bu, bi == 0, bi == len(bl) - 1))
                bstate = {}
                pvbank = {}

                def stage_S(k):
                    h, bu, first, last = blist[k]
                    c = h // 2
                    po = 64 * (h % 2)
                    if first:
                        pvbank[h] = pvb_i[0] % 3
                        pvb_i[0] += 1
                    sbk = 3 + scb_i[0] % 5
                    scb_i[0] += 1
                    col = 0
                    segs = []
                    if g < 2:
                        for u in bu:
                            for ti in range(2):
                                segs.append((u, ti, col, u["tiles"][ti][2], u["nq"]))
                                col += u["nq"]
                    else:
                        for u in bu:
                            segs.append((u, 0, col, 128, 32))
                            col += 32
                        for u in bu:
                            segs.append((u, 1, col, 32, 32))
                            col += 32
                    for si, (u, ti, c0, nk, nq) in enumerate(segs):
                        k0, ks, nk_, vt_ = u["tiles"][ti]
                        lhs = kw3[:, c, sl(k0, nk, ks)]
                        qv = qt[c][h % 2]
                        rhs = qv.ap[:, sl(u["q0"], nq, u["qs"])]
                        tk.op("pe", lambda e: e.matmul(ps[0:nk, sbk, c0:c0 + nq], lhs, rhs, start=True, stop=True),
                              R=KWg.keys + qv.keys, W=PK(sbk), sig=(si == len(segs) - 1))
                    sbf = SBF[sbf_i[0] % 2]
                    sbf_i[0] += 1
                    ptv = (PT2 if g == 2 else PT)[pt_i[0] % 3]
                    pt_i[0] += 1
                    if g < 2:
                        o_ = (g * 8 + h) * 256
                        bt = BT01.ap[:, o_:o_ + 256].unsqueeze(1).broadcast_to([128, 2, 256])
                        tk.op("dve", lambda e: e.tensor_tensor(out=sbf.ap.rearrange("p (a b) -> p a b", a=2),
                                                               in0=ps[:, sbk, :].rearrange("p (a b) -> p a b", a=2),
                                                               in1=bt, op=ALU.add),
                              R=PK(sbk) + BT01.keys, W=sbf.keys)
                        tk.op("act", lambda e: e.activation(out=ptv.ap, in_=sbf.ap, func=AF.Exp), R=sbf.keys, W=ptv.keys)
                    else:
                        btA = BT2.ap[:, h * 64:h * 64 + 32].unsqueeze(1).broadcast_to([128, 8, 32])
                        btB = BT2.ap[0:32, h * 64 + 32:h * 64 + 64].unsqueeze(1).broadcast_to([32, 8, 32])
                        tk.op("dve", lambda e: e.tensor_tensor(out=sbf.ap[:, 0:256].rearrange("p (a b) -> p a b", a=8),
                                                               in0=ps[:, sbk, 0:256].rearrange("p (a b) -> p a b", a=8),
                                                               in1=btA, op=ALU.add),
                              R=PK(sbk) + BT2.keys, W=sbf.keys)
                        tk.op("dve", lambda e: e.tensor_tensor(out=sbf.ap[0:32, 256:512].rearrange("p (a b) -> p a b", a=8),
                                                               in0=ps[0:32, sbk, 256:512].rearrange("p (a b) -> p a b", a=8),
                                                               in1=btB, op=ALU.add),
                              R=PK(sbk) + BT2.keys, W=sbf.keys)
                        tk.op("act", lambda e: e.activation(out=ptv.ap[:, 0:256], in_=sbf.ap[:, 0:256], func=AF.Exp),
                              R=sbf.keys, W=ptv.keys)
                        tk.op("act", lambda e: e.activation(out=ptv.ap[0:32, 256:512], in_=sbf.ap[0:32, 256:512], func=AF.Exp),
                              R=sbf.keys, W=ptv.keys)
                    bstate[k] = (segs, ptv)

                def stage_P(k):
                    h, bu, first, last = blist[k]
                    segs, ptv = bstate.pop(k)
                    pv = pvbank[h]
                    n = len(segs)
                    order = sorted(range(n), key=lambda i: (segs[i][1], segs[i][0]["q0"]))
                    for oi, i in enumerate(order):
                        u, ti, c0, nk, nq = segs[i]
                        k0, ks, nk_, vt_ = u["tiles"][ti]
                        nke = 128 if g == 2 else nk
                        lhs = vt_.ap[0:nke, :].rearrange("p (h e) -> p h e", h=8)[:, h, :]
                        rhs = ptv.ap[0:nke, c0:c0 + nq]
                        outap = ps[0:65, pv, sl(u["q0"], nq, u["qs"])]
                        st_ = (first and oi == 0)
                        tk.op("pe", lambda e: e.matmul(outap, lhs, rhs, start=st_, stop=(last and oi == n - 1),
                                                       skip_group_check=True),
                              R=vt_.keys + ptv.keys, W=PK(pv), sig=(oi == n - 1))
                    if not last:
                        return
                    if g == 0:
                        tk.op("act", lambda e: e.activation(out=ACC[h].ap[0:65, :], in_=ps[0:65, pv, :], func=AF.Copy),
                              R=PK(pv), W=ACC[h].keys)
                    elif g == 1:
                        tk.op("dve", lambda e: e.tensor_tensor(out=ACC[h].ap[0:65, :], in0=ps[0:65, pv, :], in1=ACC[h].ap[0:65, :],
                                                               op=ALU.add),
                              R=PK(pv) + ACC[h].keys, W=ACC[h].keys)
                    else:
                        a3 = ACC[h].ap[0:65, :].rearrange("p (i r) -> p i r", r=16)[:, :, 8 * pi:8 * pi + 8]
                        p3 = ps[0:65, pv, :].rearrange("p (i r) -> p i r", r=16)[:, :, 8 * pi:8 * pi + 8]
                        tk.op("dve", lambda e: e.tensor_tensor(out=a3, in0=p3, in1=a3, op=ALU.add),
                              R=PK(pv) + ACC[h].keys, W=ACC[h].keys)

                nbat = len(blist)
                for k in range(nbat + LA):
                    if k < nbat:
                        stage_S(k)
                    if k >= LA:
                        stage_P(k - LA)

        acc_keys = []
        for h_ in range(8):
            acc_keys += ACC[h_].keys
        den_all = av(42, 16 * KB, F32).ap[64:65, :]
        tk.op("act", lambda e: e.activation(out=den_all, in_=den_all, func=AF.Ln), R=acc_keys, W=acc_keys)
        tk.op("act", lambda e: e.activation(out=den_all, in_=den_all, func=AF.Exp, scale=-1.0), R=acc_keys, W=acc_keys)

        def bc_head(h):
            b = nb()
            tk.op("pe", lambda e: e.matmul(ps[0:64, b, :], ONE.ap[64:65, 0:64], ACC[h].ap[64:65, :], start=True, stop=True),
                  R=ONE.keys + ACC[h].keys, W=PK(b))
            tk.op("dve", lambda e: e.tensor_tensor(out=YAT[h].ap[0:64, :], in0=ps[0:64, b, :], in1=ACC[h].ap[0:64, :], op=ALU.mult),
                  R=PK(b) + ACC[h].keys, W=YAT[h].keys)

        def ep_gb(oc, b):
            tk.op("act", lambda e: e.activation(out=SBG[oc].ap, in_=PS(b), func=AF.Sigmoid), R=PK(b), W=SBG[oc].keys)
            bc_head(oc)
        fm_proj(wb_in, "in", COL_GB, XNT, 8, ep_gb)


        def ep_mt(oc, b):
            t2 = T12[2]
            tk.op("dve", lambda e: e.tensor_tensor(out=t2.ap, in0=PS(b), in1=SBG[oc].ap, op=ALU.mult),
                  R=PK(b) + SBG[oc].keys, W=t2.keys)
            tk.op("dve" if j == 0 else "pool", lambda e: e.tensor_tensor(out=MT[oc].ap, in0=t2.ap, in1=M1[oc].ap, op=ALU.add),
                  R=t2.keys + M1[oc].keys, W=MT[oc].keys)
        for half in range(2):
            s = ws_i[0] % 3
            ws_i[0] += 1
            dst = WS[s].ap[0:64, 0:8 * 512].rearrange("p (k c) -> p k c", k=8)
            src = wb_b[:, 512 * half:512 * half + 512].rearrange("(k p) c -> p k c", p=64)
            tk.dma("sp", dst, src, R=[("w", "b", half)], W=WS[s].keys, stream="w%d" % s)
            for q in range(4):
                oc = half * 4 + q
                b = nb()
                for hh in range(8):
                    tk.op("pe", lambda e: e.matmul(PS(b), dst[:, hh, q * 128:(q + 1) * 128], YAT[hh].ap[0:64, :],
                                                   start=(hh == 0), stop=(hh == 7)),
                          R=WS[s].keys + YAT[hh].keys, W=PK(b), sig=(hh == 7))
                ep_mt(oc, b)

        wo = [load_w(wb_out, "out", 512 * hf, 512) for hf in range(2)]
        load_x(j)
        if j + 1 < NOWN // T:
            load_x(j + 1, XP_all, "xp")
        for tb in range(4):
            for hf in range(2):
                b = nb()
                wsv, wap = wo[hf]
                for kc in range(8):
                    tk.op("pe", lambda e: e.matmul(PS(b), MT[kc].ap[:, tb * 128:(tb + 1) * 128], wap[:, kc, :],
                                                   start=(kc == 0), stop=(kc == 7)),
                          R=wsv.keys + MT[kc].keys, W=PK(b), sig=(kc == 7))
                xa = XH[tb].ap[:, hf * 512:(hf + 1) * 512]
                tk.op("dve", lambda e: e.tensor_tensor(out=xa, in0=PS(b), in1=xa, op=ALU.add),
                      R=PK(b) + XH[tb].keys, W=XH[tb].keys)

        norm_T(GFFN)
        use_pool_pow[0] = True
        def load_gu(fc):
            s_ = ws_i[0] % 3
            ws_i[0] += 1
            dstg = WS[s_].ap[:, 0:2048].rearrange("p (k c) -> p k c", k=8)
            dstu = WS[s_].ap[:, 2048:4096].rearrange("p (k c) -> p k c", k=8)
            cg = (fc * 128) // 512
            tk.dma("sp", dstg, wb_gate[:, fc * 128:fc * 128 + 256].rearrange("(k p) c -> p k c", p=128),
                   R=[("w", "gate", cg)], W=WS[s_].keys, stream="w%d" % s_)
            tk.dma("sp", dstu, wb_up[:, fc * 128:fc * 128 + 256].rearrange("(k p) c -> p k c", p=128),
                   R=[("w", "up", cg)], W=WS[s_].keys, stream="w%d" % s_)
            return WS[s_], dstg, dstu

        gu = {0: load_gu(0), 2: load_gu(2)}
        for fc in range(0, NFC, 2):
            wsv, wgap, wuap = gu.pop(fc)
            if fc + 4 < NFC:
                gu[fc + 4] = load_gu(fc + 4)
            for q in range(2):
                bg = nb()
                bu = nb()
                for (bb, wap) in ((bg, wgap), (bu, wuap)):
                    for kc in range(8):
                        tk.op("pe", lambda e: e.matmul(PS(bb), wap[:, kc, q * 128:(q + 1) * 128], XNT[kc].ap,
                                                       start=(kc == 0), stop=(kc == 7)),
                              R=wsv.keys + XNT[kc].keys, W=PK(bb), sig=(kc == 7))
                sg = SG[(fc + q) % 2]
                tk.op("act", lambda e: e.activation(out=sg.ap, in_=PS(bg), func=AF.Silu), R=PK(bg), W=sg.keys)
                tk.op("dve", lambda e: e.tensor_tensor(out=AT[fc + q].ap, in0=PS(bu), in1=sg.ap, op=ALU.mult),
                      R=PK(bu) + sg.keys, W=AT[fc + q].keys)
        for hf in range(2):
            banks = [nb() for _ in range(4)]
            f0 = 0
            while f0 < NFC:
                n = min(8, NFC - f0)
                s = ws_i[0] % 3
                ws_i[0] += 1
                dst = WS[s].ap[:, 0:n * 512].rearrange("p (k c) -> p k c", k=n)
                src = wb_down[f0 * 128:(f0 + n) * 128, hf * 512:(hf + 1) * 512].rearrange("(k p) c -> p k c", p=128)
                tk.dma("sp", dst, src, R=[("w", "down", hf)], W=WS[s].keys, stream="w%d" % s)
                for tb in range(4):
                    for q in range(n):
                        f = f0 + q
                        tk.op("pe", lambda e: e.matmul(PS(banks[tb]), AT[f].ap[:, tb * 128:(tb + 1) * 128], dst[:, q, :],
                                                       start=(f == 0), stop=(f == NFC - 1)),
                              R=WS[s].keys + AT[f].keys, W=PK(banks[tb]), sig=(q == n - 1))
                f0 += n
            if hf == 1 and j + 1 < NOWN // T:
                norm_T(GMIX, XP, tbs=(0, 1))
            for tb in range(4):
                xa = XH[tb].ap[:, hf * 512:(hf + 1) * 512]
                tk.op("dve", lambda e: e.tensor_tensor(out=xa, in0=PS(banks[tb]), in1=xa, op=ALU.add),
                      R=PK(banks[tb]) + XH[tb].keys, W=XH[tb].keys)
            if hf == 1 and j + 1 < NOWN // T:
                norm_T(GMIX, XP, tbs=(2, 3))

        for tb in range(4):
            rstd_of(XH[tb], tb, 1.0 / D)
            tk.op("dve", lambda e: e.scalar_tensor_tensor(out=XH[tb].ap, in0=XH[tb].ap, scalar=RS[tb].ap, in1=GFIN.ap,
                                                          op0=ALU.mult, op1=ALU.mult),
                  R=XH[tb].keys + RS[tb].keys + GFIN.keys, W=XH[tb].keys)
        tk.dma("act", y_d[j * T:(j + 1) * T, :].rearrange("(tb p) f -> p tb f", p=128), v3(XH_all, 4),
               R=XH_all.keys, stream="out")

    ko = tk.stream("out")
    nc.sync.wait_ge(tk.sem[ko], tk.cnt[ko])
    return nc


def _bias_tables():
    n = 24
    slopes = np.exp2(-8.0 * np.arange(1, n + 1, dtype=np.float64) / n).reshape(3, 8)
    kk = np.arange(128)[:, None]
    bt01 = np.zeros((128, 2, 8, 256), np.float32)
    for g in range(2):
        for h in range(8):
            for t, sh in enumerate((-64, 64)):
                i = np.arange(128)[None, :]
                rel = kk + sh - i
                b = np.where(np.abs(rel) <= 64, -slopes[g, h] * DIL[g] * np.abs(rel), NEG)
                bt01[:, g, h, t * 128:(t + 1) * 128] = b
    bt2 = np.full((128, 8, 64), NEG, np.float32)
    i = np.arange(32)[None, :]
    for h in range(8):
        rel = kk - 64 - i
        bt2[:, h, 0:32] = np.where(np.abs(rel) <= 64, -slopes[2, h] * 16 * np.abs(rel), NEG)
        rel = kk + 64 - i
        bt2[:, h, 32:64] = np.where(np.abs(rel) <= 64, -slopes[2, h] * 16 * np.abs(rel), NEG)
    return bt01.reshape(128, -1), bt2.reshape(128, -1)


_NC_CACHE = {}


def kernel(x, norm_mix_g, w_in, gmlp_ln_g, gmlp_ln_b, gmlp_ws, gmlp_bs, w_branch_gmlp,
           w_branch_attn, w_out, norm_ffn_g, w_ffn_gate, w_ffn_up, w_ffn_down, norm_final_g):
    f = lambda a: np.ascontiguousarray(np.asarray(a, dtype=np.float32))
    x = f(x)
    if "nc" not in _NC_CACHE:
        _NC_CACHE["nc"] = build_program()
    nc = _NC_CACHE["nc"]
    bt01, bt2 = _bias_tables()
    idn = np.eye(128, dtype=np.float32)
    ws = f(gmlp_ws)[0]
    bs = f(gmlp_bs)[0]
    common = {
        "w_in": f(w_in)[0], "w_a": f(w_branch_gmlp)[0], "w_b": f(w_branch_attn)[0], "w_out": f(w_out)[0],
        "w_gate": f(w_ffn_gate)[0], "w_up": f(w_ffn_up)[0], "w_down": f(w_ffn_down)[0],
        "gmix": f(norm_mix_g).reshape(1, D), "gffn": f(norm_ffn_g).reshape(1, D), "gfin": f(norm_final_g).reshape(1, D),
        "lng": f(gmlp_ln_g).reshape(1, D), "lnb": f(gmlp_ln_b).reshape(1, D),
        "bt01": bt01, "bt2": bt2, "idn": idn,
    }
    in_maps = []
    for c in range(8):
        b, half = c // 2, c % 2
        if half == 0:
            xs = x[b, 0:NTOK]
            wsl, bsl = ws, bs
        else:
            xs = x[b, SEQ - NTOK:SEQ][::-1]
            wsl, bsl = ws[:, ::-1, ::-1], bs[:, ::-1]
        m = dict(common)
        m["x"] = np.ascontiguousarray(xs)
        m["wst"] = np.ascontiguousarray(np.transpose(wsl, (2, 0, 1)).reshape(128, 1024))
        m["bs"] = np.ascontiguousarray(bsl.reshape(1, D))
        in_maps.append(m)
    res = run_bass_kernel_spmd(nc, in_maps, core_ids=list(range(8)))
    out = np.empty((BATCH, SEQ, D), np.float32)
    for c in range(8):
        b, half = c // 2, c % 2
        yc = np.asarray(res.results[c]["y"], dtype=np.float32)
        if half == 0:
            out[b, 0:NOWN] = yc
        else:
            out[b, NOWN:SEQ] = yc[::-1]
    return out
```

```python
import numpy as np
import ml_dtypes
import concourse.bass as bass
import concourse.mybir as mybir
from concourse.bass_utils import run_bass_kernel_spmd

F32 = mybir.dt.float32
BF16 = mybir.dt.bfloat16
AF = mybir.ActivationFunctionType
ALU = mybir.AluOpType

D = 1024
SEQ = 8192
BATCH = 4
NOWN = 4096
NTOK = 5120
T = 512
PAD = 1024
NROW = PAD + NTOK
IN_W = 8704
COL_U, COL_V, COL_Q, COL_K, COL_VA, COL_GA, COL_GB = 0, 1024, 2048, 3584, 5120, 6656, 7680
DFF = 2816
NFC = 22
DIL = (1, 4, 16)
EPS = 1e-6
NEG = -30000.0
BLK = 256


class Trk:
    def __init__(self, nc):
        self.nc = nc
        self.eng = dict(pe=nc.tensor, act=nc.scalar, dve=nc.vector, pool=nc.gpsimd, sp=nc.sync)
        self.sem = {}
        self.cnt = {}
        self.waited = {e: {} for e in self.eng}
        for e in ("pe", "act", "dve", "pool"):
            self.sem[e] = nc.alloc_semaphore("s_" + e)
            self.cnt[e] = 0
        self.st = {}
        self.nwait = 0

    def stream(self, name):
        k = "d:" + name
        if k not in self.sem:
            self.sem[k] = self.nc.alloc_semaphore("sd_" + name)
            self.cnt[k] = 0
        return k

    def _state(self, key):
        s = self.st.get(key)
        if s is None:
            s = [None, {}]
            self.st[key] = s
        return s

    def _gather(self, R, W):
        need = {}

        def add(ev):
            if ev is None:
                return
            k, v = ev
            if v > need.get(k, 0):
                need[k] = v

        for key in R:
            add(self._state(key)[0])
        for key in W:
            s = self._state(key)
            add(s[0])
            for k, v in s[1].items():
                add((k, v))
        return need

    def _wait(self, eng, need):
        for k, v in need.items():
            if eng == "pe" and k == "pe":
                continue
            if self.waited[eng].get(k, 0) >= v:
                continue
            self.eng[eng].wait_ge(self.sem[k], v)
            self.waited[eng][k] = v
            self.nwait += 1

    def _record(self, R, W, ev):
        k, v = ev
        for key in R:
            s = self._state(key)
            if v > s[1].get(k, 0):
                s[1][k] = v
        for key in W:
            s = self._state(key)
            s[0] = ev
            s[1] = {}

    def op(self, eng, fn, R=(), W=(), sig=True):
        self._wait(eng, self._gather(R, W))
        ins = fn(self.eng[eng])
        if sig:
            self.cnt[eng] += 1
            ins.then_inc(self.sem[eng], 1)
            ev = (eng, self.cnt[eng])
        else:
            ev = (eng, self.cnt[eng] + 1)
        self._record(R, W, ev)

    def dma(self, q, out, in_, R=(), W=(), stream="ld"):
        k = self.stream(stream)
        self._wait(q, self._gather(R, W))
        self.cnt[k] += 16
        self.eng[q].dma_start(out=out, in_=in_).then_inc(self.sem[k], 16)
        self._record(R, W, (k, self.cnt[k]))


def sl(start, n, step=1):
    return slice(start, start + step * (n - 1) + 1, step)


class V:
    def __init__(self, ap, keys, stream=None):
        self.ap = ap
        self.keys = keys
        self.stream = stream


def build_program():
    nc = bass.Bass("TRN2", target_bir_lowering=False)
    tk = Trk(nc)

    def din(name, shape, dt=F32):
        return nc.dram_tensor(name, list(shape), dt, kind="ExternalInput")

    x_d = din("x", [NTOK, D])
    w_in_d = din("w_in", [D, IN_W])
    w_a_d = din("w_a", [D, D])
    w_b_d = din("w_b", [512, D])
    w_out_d = din("w_out", [D, D])
    w_gate_d = din("w_gate", [D, DFF])
    w_up_d = din("w_up", [D, DFF])
    w_down_d = din("w_down", [DFF, D])
    gmix_d = din("gmix", [1, D])
    gffn_d = din("gffn", [1, D])
    gfin_d = din("gfin", [1, D])
    lng_d = din("lng", [1, D])
    lnb_d = din("lnb", [1, D])
    bs_d = din("bs", [1, D])
    wst_d = din("wst", [128, 1024])
    bt01_d = din("bt01", [128, 2 * 8 * 256])
    bt2_d = din("bt2", [128, 8 * 64])
    idn_d = din("idn", [128, 128])
    y_d = nc.dram_tensor("y", [NOWN, D], F32, kind="ExternalOutput")

    wb_in = nc.dram_tensor("wb_in", [D, IN_W], BF16, kind="Internal")
    wb_a = nc.dram_tensor("wb_a", [D, D], BF16, kind="Internal")
    wb_b = nc.dram_tensor("wb_b", [512, D], BF16, kind="Internal")
    wb_out = nc.dram_tensor("wb_out", [D, D], BF16, kind="Internal")
    wb_gate = nc.dram_tensor("wb_gate", [D, DFF], BF16, kind="Internal")
    wb_up = nc.dram_tensor("wb_up", [D, DFF], BF16, kind="Internal")
    wb_down = nc.dram_tensor("wb_down", [DFF, D], BF16, kind="Internal")
    kscr = nc.dram_tensor("kscr", [3, 4, 128, NROW], BF16, kind="Internal")
    vscr = nc.dram_tensor("vscr", [3, NROW, 520], BF16, kind="Internal")

    KB = 1024
    SH = {}
    off = 0

    def shared(name, nbytes):
        nonlocal off
        SH[name] = off
        off += nbytes

    shared("WS", 3 * 8 * KB)
    shared("XH", 16 * KB)
    for nm in ("GMIX", "GFFN", "GFIN", "LG", "LB", "BSB"):
        shared(nm, 4 * KB)
    shared("BT01", 16 * KB)
    shared("BT2", 2 * KB)
    shared("WST", 2 * KB)
    shared("IDN", 512)
    shared("IDNB", 256)
    shared("ONE", 256)
    shared("SML", 1024)
    AR = off
    ARENA_BYTES = int(121.5 * KB)
    TOTAL = AR + ARENA_BYTES
    assert TOTAL <= 212860, TOTAL
    sb = nc.alloc_sbuf_tensor("sb", [128, TOTAL // 2], BF16)

    def view(off_b, nbytes, dt, keys=None, name=None):
        ap = sb[:, off_b // 2:(off_b + nbytes) // 2]
        if dt == F32:
            ap = ap.bitcast(F32)
        if keys is None:
            if name is not None:
                keys = [("n", name)]
            else:
                keys = [("b", i) for i in range(off_b // BLK, (off_b + nbytes - 1) // BLK + 1)]
        return V(ap, keys)

    def av(off_kb, nbytes, dt):
        return view(AR + int(off_kb * KB), nbytes, dt)

    WS = [view(SH["WS"] + i * 8 * KB, 8 * KB, BF16, name="ws%d" % i) for i in range(3)]
    XH = [view(SH["XH"] + tb * 4 * KB, 4 * KB, F32, name="xh%d" % tb) for tb in range(4)]
    XH_all = view(SH["XH"], 16 * KB, F32, keys=[("n", "xh%d" % tb) for tb in range(4)])
    GMIX = view(SH["GMIX"], 4 * KB, F32, name="gmix")
    GFFN = view(SH["GFFN"], 4 * KB, F32, name="gffn")
    GFIN = view(SH["GFIN"], 4 * KB, F32, name="gfin")
    LG = view(SH["LG"], 4 * KB, F32, name="lg")
    LB = view(SH["LB"], 4 * KB, F32, name="lb")
    BSB = view(SH["BSB"], 4 * KB, F32, name="bsb")
    BT01 = view(SH["BT01"], 16 * KB, F32, name="bt01")
    BT2 = view(SH["BT2"], 2 * KB, F32, name="bt2")
    WST = view(SH["WST"], 2 * KB, BF16, name="wst")
    IDN = view(SH["IDN"], 512, F32, name="idn")
    IDNB = view(SH["IDNB"], 256, BF16, name="idnb")
    ONE = view(SH["ONE"], 256, F32, name="one")
    SML = view(SH["SML"], 1024, F32, name="sml")
    sml_next = [0]

    def small(ncols, name):
        c0 = sml_next[0]
        sml_next[0] += ncols
        assert sml_next[0] <= 256
        return V(SML.ap[:, c0:c0 + ncols], [("n", "sml_" + name)])

    XS = [av(0, 2 * KB, BF16), av(2, 2 * KB, BF16)]
    XSF = av(4, 4 * KB, F32)
    JNK = av(8, 2 * KB, BF16)
    XNT = [av(10 + kc, KB, BF16) for kc in range(8)]
    XNT3 = av(10, 8 * KB, BF16)
    M1 = [av(18 + 2 * oc, 2 * KB, F32) for oc in range(8)]
    GU = [av(34 + oc, KB, BF16) for oc in range(8)]
    QT = [[[av((34 if s == 0 else 0) + 2 * c + v, KB, BF16) for v in range(2)] for c in range(4)] for s in range(2)]
    KST = av(34, 4 * KB, BF16)
    VSTG = [av(42 + i * 1.25, 1040, BF16) for i in range(4)]
    VN = [av(42 + 2 * tb, 2 * KB, BF16) for tb in range(4)]
    ACC = [av(42 + 2 * h, 2 * KB, F32) for h in range(8)]
    VST = [av(50, 4 * KB, F32), av(54, 4 * KB, F32)]
    SA = [av(58 + oc, KB, BF16) for oc in range(8)]
    SBF = [av(58, 2 * KB, F32), av(60, 2 * KB, F32)]
    PT = [av(62 + i, KB, BF16) for i in range(3)]
    KW = av(66, 20 * KB, BF16)
    YAT = [av(66 + h, KB, BF16) for h in range(8)]
    SBG = [av(74 + oc, KB, BF16) for oc in range(8)]
    AT = [av(66 + fc, KB, BF16) for fc in range(NFC)]
    VA = [view(AR + 86 * KB + i * 1040, 1040, BF16) for i in range(8)]
    VB = [view(AR + 86 * KB + 8320 + i * 1040, 1040, BF16) for i in range(8)]
    MT = [av(0 + oc, KB, BF16) for oc in range(8)]
    PT2 = [av(118.5 + i, KB, BF16) for i in range(3)]
    T12 = [av(102.5 + 2 * i, 2 * KB, F32) for i in range(3)]
    SG = [T12[0], T12[1]]
    KW1 = av(108.5, 8 * KB, BF16)
    XP = [av(18 + 4 * tb, 4 * KB, F32) for tb in range(4)]
    XP_all = av(18, 16 * KB, F32)
    VA_all = view(AR + 86 * KB, 8320, BF16)
    VB_all = view(AR + 86 * KB + 8320, 8320, BF16)
    ZERO = av(0, 8320, BF16)

    ps = nc.alloc_psum_tensor("ps", [128, 8, 512], F32)
    bank_i = [0]

    def nb():
        b = bank_i[0] % 8
        bank_i[0] += 1
        return b

    def PS(b):
        return ps[:, b, :]

    def PK(b):
        return [("ps", b)]

    def v3(v, a):
        return v.ap.rearrange("p (a b) -> p a b", a=a)

    def ld(out_v, in_ap, stream="c", extraR=()):
        tk.dma("sp", out_v.ap, in_ap, R=list(extraR), W=out_v.keys, stream=stream)

    consts = [GMIX, GFFN, GFIN, LG, LB, BSB, BT01, BT2, IDN, XSF]
    for vv, dd in ((GMIX, gmix_d), (GFFN, gffn_d), (GFIN, gfin_d), (LG, lng_d), (LB, lnb_d), (BSB, bs_d)):
        ld(vv, dd[0:1, :].partition_broadcast(128))
    ld(BT01, bt01_d[:, :])
    ld(BT2, bt2_d[:, :])
    ld(IDN, idn_d[:, :])
    ld(XSF, wst_d[:, :])
    kc_ = tk.stream("c")
    for vv in consts:
        for key in vv.keys:
            tk._state(key)[0] = (kc_, tk.cnt[kc_])
    tk.op("dve", lambda e: e.tensor_copy(out=WST.ap, in_=XSF.ap), R=XSF.keys, W=WST.keys)
    tk.op("dve", lambda e: e.tensor_copy(out=IDNB.ap, in_=IDN.ap), R=IDN.keys, W=IDNB.keys)
    tk.op("dve", lambda e: e.memset(ONE.ap, 1.0), W=ONE.keys)
    MHALF = small(1, "mhalf")
    SEO = [small(1, "seo0"), small(1, "seo1")]
    tk.op("pool", lambda e: e.memset(SEO[0].ap[0:64, :], 0.125), W=SEO[0].keys)
    tk.op("pool", lambda e: e.memset(SEO[0].ap[64:128, :], 0.0), W=SEO[0].keys)
    tk.op("pool", lambda e: e.memset(SEO[1].ap[0:64, :], 0.0), W=SEO[1].keys)
    tk.op("pool", lambda e: e.memset(SEO[1].ap[64:128, :], 0.125), W=SEO[1].keys)
    tk.op("pool", lambda e: e.memset(MHALF.ap, -0.5), W=MHALF.keys)

    tk.op("pool", lambda e: e.memset(ZERO.ap, 0.0), W=ZERO.keys)
    KV_KEYS = [("dr", "kv", i) for i in range(8)]
    for g in range(3):
        tk.dma("sp", vscr[g, 0:PAD, :].rearrange("(p a) c -> p (a c)", a=8), ZERO.ap,
               R=ZERO.keys, W=KV_KEYS[0:1], stream="kvz")
        tk.dma("sp", kscr[g, :, :, 0:PAD].rearrange("c p t -> p c t"),
               ZERO.ap[:, 0:4096].rearrange("p (c t) -> p c t", c=4),
               R=ZERO.keys, W=KV_KEYS[0:1], stream="kvz")
    kz_ = tk.stream("kvz")
    for key in ZERO.keys:
        tk._state(key)[1][kz_] = tk.cnt[kz_]
    for pt_ in PT2:
        tk.op("pool", lambda e: e.memset(pt_.ap, 0.0), W=pt_.keys)
    tk.op("pool", lambda e: e.memset(VB_all.ap, 0.0), W=VB_all.keys)
    for i in range(4):
        tk.op("pool", lambda e, i=i: e.memset(VSTG[i].ap, 1.0), W=VSTG[i].keys)

    conv_jobs = []

    def convert(src, dst, name, rows, c0, c1):
        c = c0
        while c < c1:
            w = min(2048, c1 - c)
            keys = [("w", name, cg) for cg in range(c // 512, (c + w - 1) // 512 + 1)]
            for r in range(0, rows, 128):
                conv_jobs.append((dst[r:r + 128, c:c + w], src[r:r + 128, c:c + w], keys, "cv_%s_%d" % (name, c)))
            c += w

    def issue_conv(n, ffn=True):
        while n > 0 and conv_jobs:
            st0 = conv_jobs[0][3]
            if (not ffn) and st0.split("_")[1] in ("gate", "up", "down"):
                return
            while conv_jobs and conv_jobs[0][3] == st0:
                o, i, keys, st = conv_jobs.pop(0)
                tk.dma("pool", o, i, W=keys, stream=st)
                n -= 1
        return
        for _ in range(0):
            o, i, keys, st = conv_jobs.pop(0)
            tk.dma("pool", o, i, W=keys, stream=st)

    convert(w_in_d, wb_in, "in", D, COL_K, COL_GA)
    convert(w_in_d, wb_in, "in", D, 0, COL_K)
    convert(w_in_d, wb_in, "in", D, COL_GA, IN_W)
    convert(w_a_d, wb_a, "a", D, 0, D)
    convert(w_b_d, wb_b, "b", 512, 0, D)
    convert(w_out_d, wb_out, "out", D, 0, D)
    convert(w_gate_d, wb_gate, "gate", D, 0, 2048)
    convert(w_up_d, wb_up, "up", D, 0, 2048)
    convert(w_gate_d, wb_gate, "gate", D, 2048, DFF)
    convert(w_up_d, wb_up, "up", D, 2048, DFF)
    convert(w_down_d, wb_down, "down", DFF, 0, D)
    issue_conv(16)

    ws_i = [0]

    def load_w(dram, name, c0, ncols, krows=D):
        s = ws_i[0] % 3
        ws_i[0] += 1
        nkc = krows // 128
        dst = WS[s].ap[:, 0:nkc * ncols].rearrange("p (k c) -> p k c", k=nkc)
        src = dram[0:krows, c0:c0 + ncols].rearrange("(k p) c -> p k c", p=128)
        keys = [("w", name, cg) for cg in range(c0 // 512, (c0 + ncols - 1) // 512 + 1)]
        tk.dma("sp", dst, src, R=keys, W=WS[s].keys, stream="w%d" % s)
        return WS[s], dst

    SS = [small(1, "ss%d" % i) for i in range(4)]
    TA = [small(1, "ta%d" % i) for i in range(4)]
    RS = [small(1, "rs%d" % i) for i in range(4)]
    TB = [small(1, "tb%d" % i) for i in range(4)]
    use_pool_pow = [False]

    def rstd_of(src_v, i, n_inv):
        tk.op("act", lambda e: e.activation(out=JNK.ap, in_=src_v.ap, func=AF.Square, accum_out=SS[i].ap),
              R=src_v.keys, W=JNK.keys + SS[i].keys)
        tk.op("dve", lambda e: e.tensor_scalar(out=TA[i].ap, in0=SS[i].ap, scalar1=n_inv, scalar2=EPS,
                                               op0=ALU.mult, op1=ALU.add), R=SS[i].keys, W=TA[i].keys)
        if use_pool_pow[0]:
            tk.op("pool", lambda e: e.tensor_tensor(out=RS[i].ap, in0=TA[i].ap, in1=MHALF.ap, op=ALU.pow),
                  R=TA[i].keys + MHALF.keys, W=RS[i].keys)
        else:
            tk.op("act", lambda e: e.activation(out=TB[i].ap, in_=TA[i].ap, func=AF.Sqrt), R=TA[i].keys, W=TB[i].keys)
            tk.op("dve", lambda e: e.reciprocal(out=RS[i].ap, in_=TB[i].ap), R=TB[i].keys, W=RS[i].keys)

    def norm_scale(gain_v, SRC, tb, xs):
        tk.op("dve", lambda e: e.scalar_tensor_tensor(out=xs.ap, in0=SRC[tb].ap, scalar=RS[tb].ap, in1=gain_v.ap,
                                                      op0=ALU.mult, op1=ALU.mult),
              R=SRC[tb].keys + RS[tb].keys + gain_v.keys, W=xs.keys)

    def norm_xpose(tb, xs, XNT3_, XNT_):
        for hf in range(2):
            b = nb()
            for q in range(4):
                kc = hf * 4 + q
                tk.op("pe", lambda e: e.transpose(out=ps[:, b, :].bitcast(BF16)[:, q * 128:(q + 1) * 128],
                                                  in_=xs.ap[:, kc * 128:(kc + 1) * 128], identity=IDNB.ap),
                      R=xs.keys + IDNB.keys, W=PK(b), sig=(q == 3))
            outap = v3(XNT3_, 8)[:, hf * 4:hf * 4 + 4, tb * 128:(tb + 1) * 128]
            inap = ps[:, b, :].bitcast(BF16)[:, 0:512].rearrange("p (a b) -> p a b", a=4)
            wk = []
            for kc in range(hf * 4, hf * 4 + 4):
                wk += XNT_[kc].keys
            tk.op("act", lambda e: e.activation(out=outap, in_=inap, func=AF.Copy), R=PK(b), W=wk)

    def norm_T(gain_v, SRC=None, tbs=(0, 1, 2, 3), DST=None):
        if SRC is None:
            SRC = XH
        XNT3_, XNT_ = (XNT3, XNT) if DST is None else DST
        for tb in tbs:
            rstd_of(SRC[tb], tb, 1.0 / D)
        for tb in tbs:
            xs = XS[tb % 2]
            norm_scale(gain_v, SRC, tb, xs)
            norm_xpose(tb, xs, XNT3_, XNT_)

    def load_x(j, dst=None, stream="x"):
        if dst is None:
            dst = XH_all
        tk.dma("sp", v3(dst, 4), x_d[j * T:(j + 1) * T, :].rearrange("(tb p) f -> p tb f", p=128),
               W=dst.keys, stream=stream)

    LA = 2
    jobs = []
    for j in range(NTOK // T):
        for g in ((0, 1, 2) if j < 9 else (2,)):
            jobs.append((j, g, "K"))
            jobs.append((j, g, "V"))
    wq = {}

    def wload(idx):
        if idx < len(jobs) and idx not in wq:
            j_, g_, kind_ = jobs[idx]
            wq[idx] = load_w(wb_in, "in", (COL_K if kind_ == "K" else COL_VA) + 512 * g_, 512)

    XNTB = [av(50 + kc, KB, BF16) for kc in range(8)]
    XNTB3 = av(50, 8 * KB, BF16)
    XHB = [av(58 + 4 * tb, 4 * KB, F32) for tb in range(4)]
    XHB_all = av(58, 16 * KB, F32)
    xbuf = [(XH, XH_all, "x"), (XHB, XHB_all, "xb")]
    nbuf = [(XNT3, XNT), (XNTB3, XNTB)]
    NT1 = NTOK // T
    XS4 = [av(74 + 2 * i, 2 * KB, BF16) for i in range(4)]
    KSTS = [KST, av(82, 4 * KB, BF16)]
    kst_i = [0]
    load_x(0, xbuf[0][1], xbuf[0][2])
    wload(0)
    wload(1)
    norm_T(GMIX, xbuf[0][0], DST=nbuf[0])
    cur_tile = -1
    job_in_tile = 0
    for idx, (j, g, kind) in enumerate(jobs):
        if j != cur_tile:
            cur_tile = j
            job_in_tile = 0
            if j + 1 < NT1:
                load_x(j + 1, xbuf[(j + 1) % 2][1], xbuf[(j + 1) % 2][2])
            issue_conv(8, ffn=False)
        XNTc = nbuf[j % 2][1]
        if j + 1 < NT1:
            SRCn = xbuf[(j + 1) % 2][0]
            if job_in_tile == 1:
                for tb in range(4):
                    rstd_of(SRCn[tb], tb, 1.0 / D)
                for tb in range(4):
                    norm_scale(GMIX, SRCn, tb, XS4[tb])
            if job_in_tile == 4:
                for tb in range(4):
                    norm_xpose(tb, XS4[tb], nbuf[(j + 1) % 2][0], nbuf[(j + 1) % 2][1])
        job_in_tile += 1
        wsv, wap = wq.pop(idx)
        wload(idx + 1)
        wload(idx + 2)
        if kind == "K":
            KSTc = KSTS[kst_i[0] % 2]
            kst_s = "kst%d" % (kst_i[0] % 2)
            kst_i[0] += 1
            for c in range(4):
                b = nb()
                for kc in range(8):
                    tk.op("pe", lambda e: e.matmul(PS(b), wap[:, kc, c * 128:(c + 1) * 128], XNTc[kc].ap,
                                                   start=(kc == 0), stop=(kc == 7)),
                          R=wsv.keys + XNTc[kc].keys, W=PK(b), sig=(kc == 7))
                tk.op("act", lambda e: e.activation(out=KSTc.ap[:, c * 512:(c + 1) * 512], in_=PS(b), func=AF.Copy),
                      R=PK(b), W=KSTc.keys)
            tk.dma("act", kscr[g, :, :, PAD + j * T:PAD + (j + 1) * T].rearrange("c p t -> p c t"),
                   v3(KSTc, 4), R=KSTc.keys, W=[("dr", "kv", 6 + (kst_i[0] - 1) % 2)], stream=kst_s)
        else:
            for tb in range(4):
                b = nb()
                for kc in range(8):
                    tk.op("pe", lambda e: e.matmul(PS(b), XNTc[kc].ap[:, tb * 128:(tb + 1) * 128], wap[:, kc, :],
                                                   start=(kc == 0), stop=(kc == 7)),
                          R=wsv.keys + XNTc[kc].keys, W=PK(b), sig=(kc == 7))
                vs = VSTG[tb]
                tk.op("dve", lambda e: e.tensor_copy(out=v3(vs, 8)[:, :, 0:64],
                                                     in_=ps[:, b, :].rearrange("p (h e) -> p h e", h=8)),
                      R=PK(b), W=vs.keys)
                r0 = PAD + j * T + tb * 128
                tk.dma("act", vscr[g, r0:r0 + 128, :], vs.ap, R=vs.keys, W=KV_KEYS[2 + tb:3 + tb], stream="vst%d" % tb)

    issue_conv(len(conv_jobs), ffn=False)
    issue_conv(len(conv_jobs))
    ST6 = [small(12, "st6_%d" % i) for i in range(2)]
    MV = [small(2, "mv%d" % i) for i in range(2)]
    pvb_i = [0]
    scb_i = [0]
    pt_i = [0]
    sbf_i = [0]

    def fm_proj(dram, name, c0, rhs_list, nk, epilogue, krows=D, k64=False):
        for half in range(2):
            wsv, wap = load_w(dram, name, c0 + 512 * half, 512, krows=krows)
            for q in range(4):
                oc = half * 4 + q
                b = nb()
                for kc in range(nk):
                    rv = rhs_list[kc]
                    if k64:
                        lhs = wap[0:64, kc, q * 128:(q + 1) * 128]
                        rhs = rv.ap[0:64, :]
                    else:
                        lhs = wap[:, kc, q * 128:(q + 1) * 128]
                        rhs = rv.ap
                    tk.op("pe", lambda e: e.matmul(PS(b), lhs, rhs, start=(kc == 0), stop=(kc == nk - 1)),
                          R=wsv.keys + rv.keys, W=PK(b), sig=(kc == nk - 1))
                epilogue(oc, b)

    for j in range(NOWN // T):
        if j == 0:
            load_x(0, XP_all, "xp")
            norm_T(GMIX, XP)

        wv = [load_w(wb_in, "in", COL_V + 512 * hf, 512) for hf in range(2)]
        for tb in range(4):
            vst = VST[tb % 2]
            st6 = ST6[tb % 2]
            mv = MV[tb % 2]
            for hf in range(2):
                b = nb()
                wsv, wap = wv[hf]
                for kc in range(8):
                    tk.op("pe", lambda e: e.matmul(PS(b), XNT[kc].ap[:, tb * 128:(tb + 1) * 128], wap[:, kc, :],
                                                   start=(kc == 0), stop=(kc == 7)),
                          R=wsv.keys + XNT[kc].keys, W=PK(b), sig=(kc == 7))
                tk.op("act", lambda e: e.activation(out=vst.ap[:, hf * 512:(hf + 1) * 512], in_=PS(b), func=AF.Gelu),
                      R=PK(b), W=vst.keys)
                tk.op("dve", lambda e: e.bn_stats(out=st6.ap[:, hf * 6:(hf + 1) * 6], in_=vst.ap[:, hf * 512:(hf + 1) * 512]),
                      R=vst.keys, W=st6.keys)
            tk.op("dve", lambda e: e.bn_aggr(out=mv.ap, in_=st6.ap), R=st6.keys, W=mv.keys)
            tk.op("dve", lambda e: e.tensor_scalar(out=TA[tb].ap, in0=mv.ap[:, 1:2], scalar1=EPS, scalar2=None, op0=ALU.add),
                  R=mv.keys, W=TA[tb].keys)
            pe_ = "dve" if j == 0 else "pool"
            if j == 0:
                tk.op("act", lambda e: e.activation(out=TB[tb].ap, in_=TA[tb].ap, func=AF.Sqrt), R=TA[tb].keys, W=TB[tb].keys)
                tk.op("dve", lambda e: e.reciprocal(out=RS[tb].ap, in_=TB[tb].ap), R=TB[tb].keys, W=RS[tb].keys)
            else:
                tk.op("pool", lambda e: e.tensor_tensor(out=RS[tb].ap, in0=TA[tb].ap, in1=MHALF.ap, op=ALU.pow),
                      R=TA[tb].keys + MHALF.keys, W=RS[tb].keys)
            tk.op("dve", lambda e: e.tensor_scalar(out=vst.ap, in0=vst.ap, scalar1=mv.ap[:, 0:1], scalar2=RS[tb].ap,
                                                   op0=ALU.subtract, op1=ALU.mult),
                  R=vst.keys + mv.keys + RS[tb].keys, W=vst.keys)
            tk.op(pe_, lambda e: e.tensor_tensor(out=vst.ap, in0=vst.ap, in1=LG.ap, op=ALU.mult),
                  R=vst.keys + LG.keys, W=vst.keys)
            tk.op(pe_, lambda e: e.tensor_tensor(out=VN[tb].ap, in0=vst.ap, in1=LB.ap, op=ALU.add),
                  R=vst.keys + LB.keys, W=VN[tb].keys)

        def ep_u(oc, b):
            tk.op("act", lambda e: e.activation(out=GU[oc].ap, in_=PS(b), func=AF.Gelu), R=PK(b), W=GU[oc].keys)
        fm_proj(wb_in, "in", COL_U, XNT, 8, ep_u)

        def spatial_group(g):
            b = nb()
            for tb in range(4):
                tk.op("pe", lambda e: e.matmul(ps[:, b, tb * 128:(tb + 1) * 128], VN[tb].ap[:, g * 128:(g + 1) * 128],
                                               WST.ap[:, g * 128:(g + 1) * 128], start=True, stop=True),
                      R=VN[tb].keys + WST.keys, W=PK(b), sig=(tb == 3))
            t1 = T12[g % 2]
            bsap = BSB.ap[:, g * 128:(g + 1) * 128].unsqueeze(1).broadcast_to([128, 4, 128])
            tk.op("dve", lambda e: e.tensor_tensor(out=v3(t1, 4), in0=ps[:, b, :].rearrange("p (a b) -> p a b", a=4),
                                                   in1=bsap, op=ALU.add),
                  R=PK(b) + BSB.keys, W=t1.keys)
            tk.op("dve" if j == 0 else "pool", lambda e: e.tensor_tensor(out=GU[g].ap, in0=t1.ap, in1=GU[g].ap, op=ALU.mult),
                  R=t1.keys + GU[g].keys, W=GU[g].keys)

        def ep_ga(oc, b):
            tk.op("act", lambda e: e.activation(out=SA[oc].ap, in_=PS(b), func=AF.Sigmoid), R=PK(b), W=SA[oc].keys)
            spatial_group(oc)
        fm_proj(wb_in, "in", COL_GA, XNT, 8, ep_ga)

        def ep_m1(oc, b):
            tk.op("dve", lambda e: e.tensor_tensor(out=M1[oc].ap, in0=PS(b), in1=SA[oc].ap, op=ALU.mult),
                  R=PK(b) + SA[oc].keys, W=M1[oc].keys)
        fm_proj(wb_a, "a", 0, GU, 8, ep_m1)

        for g in range(3):
            d = DIL[g]
            qt = QT[g % 2]
            wsv, wap = load_w(wb_in, "in", COL_Q + 512 * g, 512)
            for c in range(4):
                b = nb()
                for kc in range(8):
                    tk.op("pe", lambda e: e.matmul(PS(b), wap[:, kc, c * 128:(c + 1) * 128], XNT[kc].ap,
                                                   start=(kc == 0), stop=(kc == 7)),
                          R=wsv.keys + XNT[kc].keys, W=PK(b), sig=(kc == 7))
                for v_ in range(2):
                    tk.op("dve", lambda e: e.tensor_scalar(out=qt[c][v_].ap, in0=PS(b), scalar1=SEO[v_].ap, scalar2=None,
                                                           op0=ALU.mult),
                          R=PK(b) + SEO[v_].keys, W=qt[c][v_].keys)
            Wg = T + 128 * d
            wstart = PAD + j * T - 64 * d
            KWg = KW1 if g == 1 else KW
            kw3 = KWg.ap[:, 0:4 * Wg].rearrange("p (c t) -> p c t", c=4)
            tk.dma("sp", kw3, kscr[g, :, :, wstart:wstart + Wg].rearrange("c p t -> p c t"),
                   R=KV_KEYS, W=KWg.keys, stream=("kw1" if g == 1 else "kw"))

            base = PAD + j * T

            def mk_units(pi):
                units = []
                if g == 0:
                    tk.dma("sp", VA_all.ap[:, 0:5 * 520].rearrange("p (m c) -> p m c", m=5),
                           vscr[g, base - 64:base - 64 + 640, :].rearrange("(m p) c -> p m c", p=128),
                           R=KV_KEYS, W=VA_all.keys, stream="va_a")
                    for qb in range(4):
                        units.append(dict(q0=128 * qb, qs=1, nq=128,
                                          tiles=[(128 * qb, 1, 128, VA[qb]), (128 * qb + 128, 1, 128, VA[qb + 1])]))
                elif g == 1:
                    ka = []
                    kb = []
                    for i_ in range(4):
                        ka += VA[i_].keys
                        kb += VA[4 + i_].keys
                    tk.dma("sp", VA_all.ap[:, 0:4 * 520],
                           vscr[g, base - 256:base - 256 + 512, :].rearrange("(p r) c -> p (r c)", r=4),
                           R=KV_KEYS, W=ka, stream="va_a")
                    tk.dma("sp", VA_all.ap[:, 4 * 520:8 * 520],
                           vscr[g, base + 256:base + 256 + 512, :].rearrange("(p r) c -> p (r c)", r=4),
                           R=KV_KEYS, W=kb, stream="va_b")
                    for r in range(4):
                        units.append(dict(q0=r, qs=4, nq=128, tiles=[(r, 4, 128, VA[r]), (r + 512, 4, 128, VA[4 + r])]))
                else:
                    tk.dma("sp", VA_all.ap.rearrange("p (r c) -> p r c", r=8),
                           vscr[g, base - 1024:base - 1024 + 2048, :].rearrange("(p r) c -> p r c", r=16)[:, 8 * pi:8 * pi + 8, :],
                           R=KV_KEYS, W=VA_all.keys, stream="va_a")
                    tk.dma("sp", VB_all.ap[0:32, :].rearrange("p (r c) -> p r c", r=8),
                           vscr[g, base + 1024:base + 1024 + 512, :].rearrange("(p r) c -> p r c", r=16)[:, 8 * pi:8 * pi + 8, :],
                           R=KV_KEYS, W=VB_all.keys, stream="vb")
                    for rr in range(8):
                        r = 8 * pi + rr
                        units.append(dict(q0=r, qs=16, nq=32, tiles=[(r, 16, 128, VA[rr]), (r + 2048, 16, 32, VB[rr])]))
                return units

            for pi in range(2 if g == 2 else 1):
                units = mk_units(pi)
                blist = []
                for h in range(8):
                    bl = [units[0:2], units[2:4]] if g < 2 else [units]
                    for bi, bu in enumerate(bl):
                        blist.append((h, bu, bi == 0, bi == len(bl) - 1))
                bstate = {}
                pvbank = {}

                def stage_S(k):
                    h, bu, first, last = blist[k]
                    c = h // 2
                    po = 64 * (h % 2)
                    if first:
                        pvbank[h] = pvb_i[0] % 3
                        pvb_i[0] += 1
                    sbk = 3 + scb_i[0] % 5
                    scb_i[0] += 1
                    col = 0
                    segs = []
                    if g < 2:
                        for u in bu:
                            for ti in range(2):
                                segs.append((u, ti, col, u["tiles"][ti][2], u["nq"]))
                                col += u["nq"]
                    else:
                        for u in bu:
                            segs.append((u, 0, col, 128, 32))
                            col += 32
                        for u in bu:
                            segs.append((u, 1, col, 32, 32))
                            col += 32
                    for si, (u, ti, c0, nk, nq) in enumerate(segs):
                        k0, ks, nk_, vt_ = u["tiles"][ti]
                        lhs = kw3[:, c, sl(k0, nk, ks)]
                        qv = qt[c][h % 2]
                        rhs = qv.ap[:, sl(u["q0"], nq, u["qs"])]
                        tk.op("pe", lambda e: e.matmul(ps[0:nk, sbk, c0:c0 + nq], lhs, rhs, start=True, stop=True),
                              R=KWg.keys + qv.keys, W=PK(sbk), sig=(si == len(segs) - 1))
                    sbf = SBF[sbf_i[0] % 2]
                    sbf_i[0] += 1
                    ptv = (PT2 if g == 2 else PT)[pt_i[0] % 3]
                    pt_i[0] += 1
                    if g < 2:
                        o_ = (g * 8 + h) * 256
                        bt = BT01.ap[:, o_:o_ + 256].unsqueeze(1).broadcast_to([128, 2, 256])
                        tk.op("dve", lambda e: e.tensor_tensor(out=sbf.ap.rearrange("p (a b) -> p a b", a=2),
                                                               in0=ps[:, sbk, :].rearrange("p (a b) -> p a b", a=2),
                                                               in1=bt, op=ALU.add),
                              R=PK(sbk) + BT01.keys, W=sbf.keys)
                        tk.op("act", lambda e: e.activation(out=ptv.ap, in_=sbf.ap, func=AF.Exp), R=sbf.keys, W=ptv.keys)
                    else:
                        btA = BT2.ap[:, h * 64:h * 64 + 32].unsqueeze(1).broadcast_to([128, 8, 32])
                        btB = BT2.ap[0:32, h * 64 + 32:h * 64 + 64].unsqueeze(1).broadcast_to([32, 8, 32])
                        tk.op("dve", lambda e: e.tensor_tensor(out=sbf.ap[:, 0:256].rearrange("p (a b) -> p a b", a=8),
                                                               in0=ps[:, sbk, 0:256].rearrange("p (a b) -> p a b", a=8),
                                                               in1=btA, op=ALU.add),
                              R=PK(sbk) + BT2.keys, W=sbf.keys)
                        tk.op("dve", lambda e: e.tensor_tensor(out=sbf.ap[0:32, 256:512].rearrange("p (a b) -> p a b", a=8),
                                                               in0=ps[0:32, sbk, 256:512].rearrange("p (a b) -> p a b", a=8),
                                                               in1=btB, op=ALU.add),
                              R=PK(sbk) + BT2.keys, W=sbf.keys)
                        tk.op("act", lambda e: e.activation(out=ptv.ap[:, 0:256], in_=sbf.ap[:, 0:256], func=AF.Exp),
                              R=sbf.keys, W=ptv.keys)
                        tk.op("act", lambda e: e.activation(out=ptv.ap[0:32, 256:512], in_=sbf.ap[0:32, 256:512], func=AF.Exp),
                              R=sbf.keys, W=ptv.keys)
                    bstate[k] = (segs, ptv)

                def stage_P(k):
                    h, bu, first, last = blist[k]
                    segs, ptv = bstate.pop(k)
                    pv = pvbank[h]
                    n = len(segs)
                    order = sorted(range(n), key=lambda i: (segs[i][1], segs[i][0]["q0"]))
                    for oi, i in enumerate(order):
                        u, ti, c0, nk, nq = segs[i]
                        k0, ks, nk_, vt_ = u["tiles"][ti]
                        nke = 128 if g == 2 else nk
                        lhs = vt_.ap[0:nke, :].rearrange("p (h e) -> p h e", h=8)[:, h, :]
                        rhs = ptv.ap[0:nke, c0:c0 + nq]
                        outap = ps[0:65, pv, sl(u["q0"], nq, u["qs"])]
                        st_ = (first and oi == 0)
                        tk.op("pe", lambda e: e.matmul(outap, lhs, rhs, start=st_, stop=(last and oi == n - 1),
                                                       skip_group_check=True),
                              R=vt_.keys + ptv.keys, W=PK(pv), sig=(oi == n - 1))
                    if not last:
                        return
                    if g == 0:
                        tk.op("act", lambda e: e.activation(out=ACC[h].ap[0:65, :], in_=ps[0:65, pv, :], func=AF.Copy),
                              R=PK(pv), W=ACC[h].keys)
                    elif g == 1:
                        tk.op("dve", lambda e: e.tensor_tensor(out=ACC[h].ap[0:65, :], in0=ps[0:65, pv, :], in1=ACC[h].ap[0:65, :],
                                                               op=ALU.add),
                              R=PK(pv) + ACC[h].keys, W=ACC[h].keys)
                    else:
                        a3 = ACC[h].ap[0:65, :].rearrange("p (i r) -> p i r", r=16)[:, :, 8 * pi:8 * pi + 8]
                        p3 = ps[0:65, pv, :].rearrange("p (i r) -> p i r", r=16)[:, :, 8 * pi:8 * pi + 8]
                        tk.op("dve", lambda e: e.tensor_tensor(out=a3, in0=p3, in1=a3, op=ALU.add),
                              R=PK(pv) + ACC[h].keys, W=ACC[h].keys)

                nbat = len(blist)
                for k in range(nbat + LA):
                    if k < nbat:
                        stage_S(k)
                    if k >= LA:
                        stage_P(k - LA)

        acc_keys = []
        for h_ in range(8):
            acc_keys += ACC[h_].keys
        den_all = av(42, 16 * KB, F32).ap[64:65, :]
        tk.op("act", lambda e: e.activation(out=den_all, in_=den_all, func=AF.Ln), R=acc_keys, W=acc_keys)
        tk.op("act", lambda e: e.activation(out=den_all, in_=den_all, func=AF.Exp, scale=-1.0), R=acc_keys, W=acc_keys)

        def bc_head(h):
            b = nb()
            tk.op("pe", lambda e: e.matmul(ps[0:64, b, :], ONE.ap[64:65, 0:64], ACC[h].ap[64:65, :], start=True, stop=True),
                  R=ONE.keys + ACC[h].keys, W=PK(b))
            tk.op("dve", lambda e: e.tensor_tensor(out=YAT[h].ap[0:64, :], in0=ps[0:64, b, :], in1=ACC[h].ap[0:64, :], op=ALU.mult),
                  R=PK(b) + ACC[h].keys, W=YAT[h].keys)

        def ep_gb(oc, b):
            tk.op("act", lambda e: e.activation(out=SBG[oc].ap, in_=PS(b), func=AF.Sigmoid), R=PK(b), W=SBG[oc].keys)
        fm_proj(wb_in, "in", COL_GB, XNT, 8, ep_gb)
        for h_ in range(8):
            bc_head(h_)


        def ep_mt(oc, b):
            t2 = T12[2]
            tk.op("dve", lambda e: e.tensor_tensor(out=t2.ap, in0=PS(b), in1=SBG[oc].ap, op=ALU.mult),
                  R=PK(b) + SBG[oc].keys, W=t2.keys)
            tk.op("dve" if j == 0 else "pool", lambda e: e.tensor_tensor(out=MT[oc].ap, in0=t2.ap, in1=M1[oc].ap, op=ALU.add),
                  R=t2.keys + M1[oc].keys, W=MT[oc].keys)
        for half in range(2):
            s = ws_i[0] % 3
            ws_i[0] += 1
            dst = WS[s].ap[0:64, 0:8 * 512].rearrange("p (k c) -> p k c", k=8)
            src = wb_b[:, 512 * half:512 * half + 512].rearrange("(k p) c -> p k c", p=64)
            tk.dma("sp", dst, src, R=[("w", "b", half)], W=WS[s].keys, stream="w%d" % s)
            for q in range(4):
                oc = half * 4 + q
                b = nb()
                for hh in range(8):
                    tk.op("pe", lambda e: e.matmul(PS(b), dst[:, hh, q * 128:(q + 1) * 128], YAT[hh].ap[0:64, :],
                                                   start=(hh == 0), stop=(hh == 7)),
                          R=WS[s].keys + YAT[hh].keys, W=PK(b), sig=(hh == 7))
                ep_mt(oc, b)

        wo = [load_w(wb_out, "out", 512 * hf, 512) for hf in range(2)]
        load_x(j)
        if j + 1 < NOWN // T:
            load_x(j + 1, XP_all, "xp")
        for tb in range(4):
            for hf in range(2):
                b = nb()
                wsv, wap = wo[hf]
                for kc in range(8):
                    tk.op("pe", lambda e: e.matmul(PS(b), MT[kc].ap[:, tb * 128:(tb + 1) * 128], wap[:, kc, :],
                                                   start=(kc == 0), stop=(kc == 7)),
                          R=wsv.keys + MT[kc].keys, W=PK(b), sig=(kc == 7))
                xa = XH[tb].ap[:, hf * 512:(hf + 1) * 512]
                tk.op("dve", lambda e: e.tensor_tensor(out=xa, in0=PS(b), in1=xa, op=ALU.add),
                      R=PK(b) + XH[tb].keys, W=XH[tb].keys)
            rstd_of(XH[tb], tb, 1.0 / D)

        for tb in range(4):
            norm_scale(GFFN, XH, tb, XS[tb % 2])
            norm_xpose(tb, XS[tb % 2], XNT3, XNT)
        use_pool_pow[0] = True
        def load_gu(fc):
            s_ = ws_i[0] % 3
            ws_i[0] += 1
            dstg = WS[s_].ap[:, 0:2048].rearrange("p (k c) -> p k c", k=8)
            dstu = WS[s_].ap[:, 2048:4096].rearrange("p (k c) -> p k c", k=8)
            cg = (fc * 128) // 512
            tk.dma("sp", dstg, wb_gate[:, fc * 128:fc * 128 + 256].rearrange("(k p) c -> p k c", p=128),
                   R=[("w", "gate", cg)], W=WS[s_].keys, stream="w%d" % s_)
            tk.dma("sp", dstu, wb_up[:, fc * 128:fc * 128 + 256].rearrange("(k p) c -> p k c", p=128),
                   R=[("w", "up", cg)], W=WS[s_].keys, stream="w%d" % s_)
            return WS[s_], dstg, dstu

        gu = {0: load_gu(0), 2: load_gu(2)}
        for fc in range(0, NFC, 2):
            wsv, wgap, wuap = gu.pop(fc)
            if fc + 4 < NFC:
                gu[fc + 4] = load_gu(fc + 4)
            for q in range(2):
                bg = nb()
                bu = nb()
                for (bb, wap) in ((bg, wgap), (bu, wuap)):
                    for kc in range(8):
                        tk.op("pe", lambda e: e.matmul(PS(bb), wap[:, kc, q * 128:(q + 1) * 128], XNT[kc].ap,
                                                       start=(kc == 0), stop=(kc == 7)),
                              R=wsv.keys + XNT[kc].keys, W=PK(bb), sig=(kc == 7))
                sg = SG[(fc + q) % 2]
                tk.op("act", lambda e: e.activation(out=sg.ap, in_=PS(bg), func=AF.Silu), R=PK(bg), W=sg.keys)
                tk.op("dve", lambda e: e.tensor_tensor(out=AT[fc + q].ap, in0=PS(bu), in1=sg.ap, op=ALU.mult),
                      R=PK(bu) + sg.keys, W=AT[fc + q].keys)
        for hf in range(2):
            banks = [nb() for _ in range(4)]
            f0 = 0
            while f0 < NFC:
                n = min(8, NFC - f0)
                s = ws_i[0] % 3
                ws_i[0] += 1
                dst = WS[s].ap[:, 0:n * 512].rearrange("p (k c) -> p k c", k=n)
                src = wb_down[f0 * 128:(f0 + n) * 128, hf * 512:(hf + 1) * 512].rearrange("(k p) c -> p k c", p=128)
                tk.dma("sp", dst, src, R=[("w", "down", hf)], W=WS[s].keys, stream="w%d" % s)
                for tb in range(4):
                    for q in range(n):
                        f = f0 + q
                        tk.op("pe", lambda e: e.matmul(PS(banks[tb]), AT[f].ap[:, tb * 128:(tb + 1) * 128], dst[:, q, :],
                                                       start=(f == 0), stop=(f == NFC - 1)),
                              R=WS[s].keys + AT[f].keys, W=PK(banks[tb]), sig=(q == n - 1))
                f0 += n
            if hf == 1 and j + 1 < NOWN // T:
                norm_T(GMIX, XP, tbs=(0, 1))
            for tb in range(4):
                xa = XH[tb].ap[:, hf * 512:(hf + 1) * 512]
                tk.op("dve", lambda e: e.tensor_tensor(out=xa, in0=PS(banks[tb]), in1=xa, op=ALU.add),
                      R=PK(banks[tb]) + XH[tb].keys, W=XH[tb].keys)
            if hf == 1 and j + 1 < NOWN // T:
                norm_T(GMIX, XP, tbs=(2, 3))

        for tb in range(4):
            rstd_of(XH[tb], tb, 1.0 / D)
            tk.op("dve", lambda e: e.scalar_tensor_tensor(out=XH[tb].ap, in0=XH[tb].ap, scalar=RS[tb].ap, in1=GFIN.ap,
                                                          op0=ALU.mult, op1=ALU.mult),
                  R=XH[tb].keys + RS[tb].keys + GFIN.keys, W=XH[tb].keys)
        tk.dma("act", y_d[j * T:(j + 1) * T, :].rearrange("(tb p) f -> p tb f", p=128), v3(XH_all, 4),
               R=XH_all.keys, stream="out")

    ko = tk.stream("out")
    nc.sync.wait_ge(tk.sem[ko], tk.cnt[ko])
    return nc


def _bias_tables():
    n = 24
    slopes = np.exp2(-8.0 * np.arange(1, n + 1, dtype=np.float64) / n).reshape(3, 8)
    kk = np.arange(128)[:, None]
    bt01 = np.zeros((128, 2, 8, 256), np.float32)
    for g in range(2):
        for h in range(8):
            for t, sh in enumerate((-64, 64)):
                i = np.arange(128)[None, :]
                rel = kk + sh - i
                b = np.where(np.abs(rel) <= 64, -slopes[g, h] * DIL[g] * np.abs(rel), NEG)
                bt01[:, g, h, t * 128:(t + 1) * 128] = b
    bt2 = np.full((128, 8, 64), NEG, np.float32)
    i = np.arange(32)[None, :]
    for h in range(8):
        rel = kk - 64 - i
        bt2[:, h, 0:32] = np.where(np.abs(rel) <= 64, -slopes[2, h] * 16 * np.abs(rel), NEG)
        rel = kk + 64 - i
        bt2[:, h, 32:64] = np.where(np.abs(rel) <= 64, -slopes[2, h] * 16 * np.abs(rel), NEG)
    return bt01.reshape(128, -1), bt2.reshape(128, -1)


_NC_CACHE = {}


def kernel(x, norm_mix_g, w_in, gmlp_ln_g, gmlp_ln_b, gmlp_ws, gmlp_bs, w_branch_gmlp,
           w_branch_attn, w_out, norm_ffn_g, w_ffn_gate, w_ffn_up, w_ffn_down, norm_final_g):
    f = lambda a: np.ascontiguousarray(np.asarray(a, dtype=np.float32))
    x = f(x)
    if "nc" not in _NC_CACHE:
        _NC_CACHE["nc"] = build_program()
    nc = _NC_CACHE["nc"]
    bt01, bt2 = _bias_tables()
    idn = np.eye(128, dtype=np.float32)
    ws = f(gmlp_ws)[0]
    bs = f(gmlp_bs)[0]
    common = {
        "w_in": f(w_in)[0], "w_a": f(w_branch_gmlp)[0], "w_b": f(w_branch_attn)[0], "w_out": f(w_out)[0],
        "w_gate": f(w_ffn_gate)[0], "w_up": f(w_ffn_up)[0], "w_down": f(w_ffn_down)[0],
        "gmix": f(norm_mix_g).reshape(1, D), "gffn": f(norm_ffn_g).reshape(1, D), "gfin": f(norm_final_g).reshape(1, D),
        "lng": f(gmlp_ln_g).reshape(1, D), "lnb": f(gmlp_ln_b).reshape(1, D),
        "bt01": bt01, "bt2": bt2, "idn": idn,
    }
    in_maps = []
    for c in range(8):
        b, half = c // 2, c % 2
        if half == 0:
            xs = x[b, 0:NTOK]
            wsl, bsl = ws, bs
        else:
            xs = x[b, SEQ - NTOK:SEQ][::-1]
            wsl, bsl = ws[:, ::-1, ::-1], bs[:, ::-1]
        m = dict(common)
        m["x"] = np.ascontiguousarray(xs)
        m["wst"] = np.ascontiguousarray(np.transpose(wsl, (2, 0, 1)).reshape(128, 1024))
        m["bs"] = np.ascontiguousarray(bsl.reshape(1, D))
        in_maps.append(m)
    res = run_bass_kernel_spmd(nc, in_maps, core_ids=list(range(8)))
    out = np.empty((BATCH, SEQ, D), np.float32)
    for c in range(8):
        b, half = c // 2, c % 2
        yc = np.asarray(res.results[c]["y"], dtype=np.float32)
        if half == 0:
            out[b, 0:NOWN] = yc
        else:
            out[b, NOWN:SEQ] = yc[::-1]
    return out
```

```python
import numpy as np
import ml_dtypes
import concourse.bass as bass
import concourse.mybir as mybir
from concourse.bass_utils import run_bass_kernel_spmd

F32 = mybir.dt.float32
BF16 = mybir.dt.bfloat16
AF = mybir.ActivationFunctionType
ALU = mybir.AluOpType

D = 1024
SEQ = 8192
BATCH = 4
NOWN = 4096
NTOK = 5120
T = 512
PAD = 1024
NROW = PAD + NTOK
IN_W = 8704
COL_U, COL_V, COL_Q, COL_K, COL_VA, COL_GA, COL_GB = 0, 1024, 2048, 3584, 5120, 6656, 7680
DFF = 2816
NFC = 22
DIL = (1, 4, 16)
EPS = 1e-6
NEG = -30000.0
BLK = 256


class Trk:
    def __init__(self, nc):
        self.nc = nc
        self.eng = dict(pe=nc.tensor, act=nc.scalar, dve=nc.vector, pool=nc.gpsimd, sp=nc.sync)
        self.sem = {}
        self.cnt = {}
        self.waited = {e: {} for e in self.eng}
        for e in ("pe", "act", "dve", "pool"):
            self.sem[e] = nc.alloc_semaphore("s_" + e)
            self.cnt[e] = 0
        self.st = {}
        self.nwait = 0

    def stream(self, name):
        k = "d:" + name
        if k not in self.sem:
            self.sem[k] = self.nc.alloc_semaphore("sd_" + name)
            self.cnt[k] = 0
        return k

    def _state(self, key):
        s = self.st.get(key)
        if s is None:
            s = [None, {}]
            self.st[key] = s
        return s

    def _gather(self, R, W):
        need = {}

        def add(ev):
            if ev is None:
                return
            k, v = ev
            if v > need.get(k, 0):
                need[k] = v

        for key in R:
            add(self._state(key)[0])
        for key in W:
            s = self._state(key)
            add(s[0])
            for k, v in s[1].items():
                add((k, v))
        return need

    def _wait(self, eng, need):
        for k, v in need.items():
            if eng == "pe" and k == "pe":
                continue
            if self.waited[eng].get(k, 0) >= v:
                continue
            self.eng[eng].wait_ge(self.sem[k], v)
            self.waited[eng][k] = v
            self.nwait += 1

    def _record(self, R, W, ev):
        k, v = ev
        for key in R:
            s = self._state(key)
            if v > s[1].get(k, 0):
                s[1][k] = v
        for key in W:
            s = self._state(key)
            s[0] = ev
            s[1] = {}

    def op(self, eng, fn, R=(), W=(), sig=True):
        self._wait(eng, self._gather(R, W))
        ins = fn(self.eng[eng])
        if sig:
            self.cnt[eng] += 1
            ins.then_inc(self.sem[eng], 1)
            ev = (eng, self.cnt[eng])
        else:
            ev = (eng, self.cnt[eng] + 1)
        self._record(R, W, ev)

    def dma(self, q, out, in_, R=(), W=(), stream="ld"):
        k = self.stream(stream)
        self._wait(q, self._gather(R, W))
        self.cnt[k] += 16
        self.eng[q].dma_start(out=out, in_=in_).then_inc(self.sem[k], 16)
        self._record(R, W, (k, self.cnt[k]))


def sl(start, n, step=1):
    return slice(start, start + step * (n - 1) + 1, step)


class V:
    def __init__(self, ap, keys, stream=None):
        self.ap = ap
        self.keys = keys
        self.stream = stream


def build_program():
    nc = bass.Bass("TRN2", target_bir_lowering=False)
    tk = Trk(nc)

    def din(name, shape, dt=F32):
        return nc.dram_tensor(name, list(shape), dt, kind="ExternalInput")

    x_d = din("x", [NTOK, D])
    w_in_d = din("w_in", [D, IN_W])
    w_a_d = din("w_a", [D, D])
    w_b_d = din("w_b", [512, D])
    w_out_d = din("w_out", [D, D])
    w_gate_d = din("w_gate", [D, DFF])
    w_up_d = din("w_up", [D, DFF])
    w_down_d = din("w_down", [DFF, D])
    gmix_d = din("gmix", [1, D])
    gffn_d = din("gffn", [1, D])
    gfin_d = din("gfin", [1, D])
    lng_d = din("lng", [1, D])
    lnb_d = din("lnb", [1, D])
    bs_d = din("bs", [1, D])
    wst_d = din("wst", [128, 1024])
    bt01_d = din("bt01", [128, 2 * 8 * 256])
    bt2_d = din("bt2", [128, 8 * 64])
    idn_d = din("idn", [128, 128])
    y_d = nc.dram_tensor("y", [NOWN, D], F32, kind="ExternalOutput")

    wb_in = nc.dram_tensor("wb_in", [D, IN_W], BF16, kind="Internal")
    wb_a = nc.dram_tensor("wb_a", [D, D], BF16, kind="Internal")
    wb_b = nc.dram_tensor("wb_b", [512, D], BF16, kind="Internal")
    wb_out = nc.dram_tensor("wb_out", [D, D], BF16, kind="Internal")
    wb_gate = nc.dram_tensor("wb_gate", [D, DFF], BF16, kind="Internal")
    wb_up = nc.dram_tensor("wb_up", [D, DFF], BF16, kind="Internal")
    wb_down = nc.dram_tensor("wb_down", [DFF, D], BF16, kind="Internal")
    kscr = nc.dram_tensor("kscr", [3, 4, 128, NROW], BF16, kind="Internal")
    vscr = nc.dram_tensor("vscr", [3, NROW, 520], BF16, kind="Internal")

    KB = 1024
    SH = {}
    off = 0

    def shared(name, nbytes):
        nonlocal off
        SH[name] = off
        off += nbytes

    shared("WS", 3 * 8 * KB)
    shared("XH", 16 * KB)
    for nm in ("GMIX", "GFFN", "GFIN", "LG", "LB", "BSB"):
        shared(nm, 4 * KB)
    shared("BT01", 16 * KB)
    shared("BT2", 2 * KB)
    shared("WST", 2 * KB)
    shared("IDN", 512)
    shared("IDNB", 256)
    shared("ONE", 256)
    shared("SML", 1024)
    AR = off
    ARENA_BYTES = int(121.5 * KB)
    TOTAL = AR + ARENA_BYTES
    assert TOTAL <= 212860, TOTAL
    sb = nc.alloc_sbuf_tensor("sb", [128, TOTAL // 2], BF16)

    def view(off_b, nbytes, dt, keys=None, name=None):
        ap = sb[:, off_b // 2:(off_b + nbytes) // 2]
        if dt == F32:
            ap = ap.bitcast(F32)
        if keys is None:
            if name is not None:
                keys = [("n", name)]
            else:
                keys = [("b", i) for i in range(off_b // BLK, (off_b + nbytes - 1) // BLK + 1)]
        return V(ap, keys)

    def av(off_kb, nbytes, dt):
        return view(AR + int(off_kb * KB), nbytes, dt)

    WS = [view(SH["WS"] + i * 8 * KB, 8 * KB, BF16, name="ws%d" % i) for i in range(3)]
    XH = [view(SH["XH"] + tb * 4 * KB, 4 * KB, F32, name="xh%d" % tb) for tb in range(4)]
    XH_all = view(SH["XH"], 16 * KB, F32, keys=[("n", "xh%d" % tb) for tb in range(4)])
    GMIX = view(SH["GMIX"], 4 * KB, F32, name="gmix")
    GFFN = view(SH["GFFN"], 4 * KB, F32, name="gffn")
    GFIN = view(SH["GFIN"], 4 * KB, F32, name="gfin")
    LG = view(SH["LG"], 4 * KB, F32, name="lg")
    LB = view(SH["LB"], 4 * KB, F32, name="lb")
    BSB = view(SH["BSB"], 4 * KB, F32, name="bsb")
    BT01 = view(SH["BT01"], 16 * KB, F32, name="bt01")
    BT2 = view(SH["BT2"], 2 * KB, F32, name="bt2")
    WST = view(SH["WST"], 2 * KB, BF16, name="wst")
    IDN = view(SH["IDN"], 512, F32, name="idn")
    IDNB = view(SH["IDNB"], 256, BF16, name="idnb")
    ONE = view(SH["ONE"], 256, F32, name="one")
    SML = view(SH["SML"], 1024, F32, name="sml")
    sml_next = [0]

    def small(ncols, name):
        c0 = sml_next[0]
        sml_next[0] += ncols
        assert sml_next[0] <= 256
        return V(SML.ap[:, c0:c0 + ncols], [("n", "sml_" + name)])

    XS = [av(0, 2 * KB, BF16), av(2, 2 * KB, BF16)]
    XSF = av(4, 4 * KB, F32)
    JNK = av(8, 2 * KB, BF16)
    XNT = [av(10 + kc, KB, BF16) for kc in range(8)]
    XNT3 = av(10, 8 * KB, BF16)
    M1 = [av(18 + 2 * oc, 2 * KB, F32) for oc in range(8)]
    GU = [av(34 + oc, KB, BF16) for oc in range(8)]
    QT = [[[av((34 if s == 0 else 0) + 2 * c + v, KB, BF16) for v in range(2)] for c in range(4)] for s in range(2)]
    KST = av(34, 4 * KB, BF16)
    VSTG = [av(42 + i * 1.25, 1040, BF16) for i in range(4)]
    VN = [av(42 + 2 * tb, 2 * KB, BF16) for tb in range(4)]
    ACC = [av(42 + 2 * h, 2 * KB, F32) for h in range(8)]
    VST = [av(50, 4 * KB, F32), av(54, 4 * KB, F32)]
    SA = [av(58 + oc, KB, BF16) for oc in range(8)]
    SBF = [av(58, 2 * KB, F32), av(60, 2 * KB, F32)]
    PT = [av(62 + i, KB, BF16) for i in range(3)]
    KW = av(66, 20 * KB, BF16)
    YAT = [av(66 + h, KB, BF16) for h in range(8)]
    SBG = [av(74 + oc, KB, BF16) for oc in range(8)]
    AT = [av(66 + fc, KB, BF16) for fc in range(NFC)]
    VA = [view(AR + 86 * KB + i * 1040, 1040, BF16) for i in range(8)]
    VB = [view(AR + 86 * KB + 8320 + i * 1040, 1040, BF16) for i in range(8)]
    MT = [av(0 + oc, KB, BF16) for oc in range(8)]
    PT2 = [av(118.5 + i, KB, BF16) for i in range(3)]
    T12 = [av(102.5 + 2 * i, 2 * KB, F32) for i in range(3)]
    SG = [T12[0], T12[1]]
    KW1 = av(108.5, 8 * KB, BF16)
    XP = [av(18 + 4 * tb, 4 * KB, F32) for tb in range(4)]
    XP_all = av(18, 16 * KB, F32)
    VA_all = view(AR + 86 * KB, 8320, BF16)
    VB_all = view(AR + 86 * KB + 8320, 8320, BF16)
    ZERO = av(0, 8320, BF16)

    ps = nc.alloc_psum_tensor("ps", [128, 8, 512], F32)
    bank_i = [0]

    def nb():
        b = bank_i[0] % 8
        bank_i[0] += 1
        return b

    def PS(b):
        return ps[:, b, :]

    def PK(b):
        return [("ps", b)]

    def v3(v, a):
        return v.ap.rearrange("p (a b) -> p a b", a=a)

    def ld(out_v, in_ap, stream="c", extraR=()):
        tk.dma("sp", out_v.ap, in_ap, R=list(extraR), W=out_v.keys, stream=stream)

    consts = [GMIX, GFFN, GFIN, LG, LB, BSB, BT01, BT2, IDN, XSF]
    for vv, dd in ((GMIX, gmix_d), (GFFN, gffn_d), (GFIN, gfin_d), (LG, lng_d), (LB, lnb_d), (BSB, bs_d)):
        ld(vv, dd[0:1, :].partition_broadcast(128))
    ld(BT01, bt01_d[:, :])
    ld(BT2, bt2_d[:, :])
    ld(IDN, idn_d[:, :])
    ld(XSF, wst_d[:, :])
    kc_ = tk.stream("c")
    for vv in consts:
        for key in vv.keys:
            tk._state(key)[0] = (kc_, tk.cnt[kc_])
    tk.op("dve", lambda e: e.tensor_copy(out=WST.ap, in_=XSF.ap), R=XSF.keys, W=WST.keys)
    tk.op("dve", lambda e: e.tensor_copy(out=IDNB.ap, in_=IDN.ap), R=IDN.keys, W=IDNB.keys)
    tk.op("dve", lambda e: e.memset(ONE.ap, 1.0), W=ONE.keys)
    MHALF = small(1, "mhalf")
    SEO = [small(1, "seo0"), small(1, "seo1")]
    tk.op("pool", lambda e: e.memset(SEO[0].ap[0:64, :], 0.125), W=SEO[0].keys)
    tk.op("pool", lambda e: e.memset(SEO[0].ap[64:128, :], 0.0), W=SEO[0].keys)
    tk.op("pool", lambda e: e.memset(SEO[1].ap[0:64, :], 0.0), W=SEO[1].keys)
    tk.op("pool", lambda e: e.memset(SEO[1].ap[64:128, :], 0.125), W=SEO[1].keys)
    tk.op("pool", lambda e: e.memset(MHALF.ap, -0.5), W=MHALF.keys)

    tk.op("pool", lambda e: e.memset(ZERO.ap, 0.0), W=ZERO.keys)
    KV_KEYS = [("dr", "kv", i) for i in range(8)]
    for g in range(3):
        tk.dma("sp", vscr[g, 0:PAD, :].rearrange("(p a) c -> p (a c)", a=8), ZERO.ap,
               R=ZERO.keys, W=KV_KEYS[0:1], stream="kvz")
        tk.dma("sp", kscr[g, :, :, 0:PAD].rearrange("c p t -> p c t"),
               ZERO.ap[:, 0:4096].rearrange("p (c t) -> p c t", c=4),
               R=ZERO.keys, W=KV_KEYS[0:1], stream="kvz")
    kz_ = tk.stream("kvz")
    for key in ZERO.keys:
        tk._state(key)[1][kz_] = tk.cnt[kz_]
    for pt_ in PT2:
        tk.op("pool", lambda e: e.memset(pt_.ap, 0.0), W=pt_.keys)
    tk.op("pool", lambda e: e.memset(VB_all.ap, 0.0), W=VB_all.keys)
    for i in range(4):
        tk.op("pool", lambda e, i=i: e.memset(VSTG[i].ap, 1.0), W=VSTG[i].keys)

    conv_jobs = []

    def convert(src, dst, name, rows, c0, c1):
        c = c0
        while c < c1:
            w = min(2048, c1 - c)
            keys = [("w", name, cg) for cg in range(c // 512, (c + w - 1) // 512 + 1)]
            for r in range(0, rows, 128):
                conv_jobs.append((dst[r:r + 128, c:c + w], src[r:r + 128, c:c + w], keys, "cv_%s_%d" % (name, c)))
            c += w

    def issue_conv(n, ffn=True):
        while n > 0 and conv_jobs:
            st0 = conv_jobs[0][3]
            if (not ffn) and st0.split("_")[1] in ("gate", "up", "down"):
                return
            while conv_jobs and conv_jobs[0][3] == st0:
                o, i, keys, st = conv_jobs.pop(0)
                tk.dma("pool", o, i, W=keys, stream=st)
                n -= 1
        return
        for _ in range(0):
            o, i, keys, st = conv_jobs.pop(0)
            tk.dma("pool", o, i, W=keys, stream=st)

    convert(w_in_d, wb_in, "in", D, COL_K, COL_GA)
    convert(w_in_d, wb_in, "in", D, 0, COL_K)
    convert(w_in_d, wb_in, "in", D, COL_GA, IN_W)
    convert(w_a_d, wb_a, "a", D, 0, D)
    convert(w_b_d, wb_b, "b", 512, 0, D)
    convert(w_out_d, wb_out, "out", D, 0, D)
    convert(w_gate_d, wb_gate, "gate", D, 0, 2048)
    convert(w_up_d, wb_up, "up", D, 0, 2048)
    convert(w_gate_d, wb_gate, "gate", D, 2048, DFF)
    convert(w_up_d, wb_up, "up", D, 2048, DFF)
    convert(w_down_d, wb_down, "down", DFF, 0, D)
    issue_conv(16)

    ws_i = [0]

    def load_w(dram, name, c0, ncols, krows=D):
        s = ws_i[0] % 3
        ws_i[0] += 1
        nkc = krows // 128
        dst = WS[s].ap[:, 0:nkc * ncols].rearrange("p (k c) -> p k c", k=nkc)
        src = dram[0:krows, c0:c0 + ncols].rearrange("(k p) c -> p k c", p=128)
        keys = [("w", name, cg) for cg in range(c0 // 512, (c0 + ncols - 1) // 512 + 1)]
        tk.dma("sp", dst, src, R=keys, W=WS[s].keys, stream="w%d" % s)
        return WS[s], dst

    SS = [small(1, "ss%d" % i) for i in range(8)]
    TA = [small(1, "ta%d" % i) for i in range(8)]
    RS = [small(1, "rs%d" % i) for i in range(8)]
    TB = [small(1, "tb%d" % i) for i in range(8)]
    use_pool_pow = [False]

    def rstd_of(src_v, i, n_inv):
        tk.op("act", lambda e: e.activation(out=JNK.ap, in_=src_v.ap, func=AF.Square, accum_out=SS[i].ap),
              R=src_v.keys, W=JNK.keys + SS[i].keys)
        tk.op("dve", lambda e: e.tensor_scalar(out=TA[i].ap, in0=SS[i].ap, scalar1=n_inv, scalar2=EPS,
                                               op0=ALU.mult, op1=ALU.add), R=SS[i].keys, W=TA[i].keys)
        if use_pool_pow[0]:
            tk.op("pool", lambda e: e.tensor_tensor(out=RS[i].ap, in0=TA[i].ap, in1=MHALF.ap, op=ALU.pow),
                  R=TA[i].keys + MHALF.keys, W=RS[i].keys)
        else:
            tk.op("act", lambda e: e.activation(out=TB[i].ap, in_=TA[i].ap, func=AF.Sqrt), R=TA[i].keys, W=TB[i].keys)
            tk.op("dve", lambda e: e.reciprocal(out=RS[i].ap, in_=TB[i].ap), R=TB[i].keys, W=RS[i].keys)

    def norm_scale(gain_v, SRC, tb, xs):
        tk.op("dve", lambda e: e.scalar_tensor_tensor(out=xs.ap, in0=SRC[tb].ap, scalar=RS[tb].ap, in1=gain_v.ap,
                                                      op0=ALU.mult, op1=ALU.mult),
              R=SRC[tb].keys + RS[tb].keys + gain_v.keys, W=xs.keys)

    def norm_xpose(tb, xs, XNT3_, XNT_):
        for hf in range(2):
            b = nb()
            for q in range(4):
                kc = hf * 4 + q
                tk.op("pe", lambda e: e.transpose(out=ps[:, b, :].bitcast(BF16)[:, q * 128:(q + 1) * 128],
                                                  in_=xs.ap[:, kc * 128:(kc + 1) * 128], identity=IDNB.ap),
                      R=xs.keys + IDNB.keys, W=PK(b), sig=(q == 3))
            outap = v3(XNT3_, 8)[:, hf * 4:hf * 4 + 4, tb * 128:(tb + 1) * 128]
            inap = ps[:, b, :].bitcast(BF16)[:, 0:512].rearrange("p (a b) -> p a b", a=4)
            wk = []
            for kc in range(hf * 4, hf * 4 + 4):
                wk += XNT_[kc].keys
            tk.op("act", lambda e: e.activation(out=outap, in_=inap, func=AF.Copy), R=PK(b), W=wk)

    def norm_T(gain_v, SRC=None, tbs=(0, 1, 2, 3), DST=None):
        if SRC is None:
            SRC = XH
        XNT3_, XNT_ = (XNT3, XNT) if DST is None else DST
        for tb in tbs:
            rstd_of(SRC[tb], tb, 1.0 / D)
        for tb in tbs:
            xs = XS[tb % 2]
            norm_scale(gain_v, SRC, tb, xs)
            norm_xpose(tb, xs, XNT3_, XNT_)

    def load_x(j, dst=None, stream="x"):
        if dst is None:
            dst = XH_all
        tk.dma("sp", v3(dst, 4), x_d[j * T:(j + 1) * T, :].rearrange("(tb p) f -> p tb f", p=128),
               W=dst.keys, stream=stream)

    LA = 2
    jobs = []
    for j in range(NTOK // T):
        for g in ((0, 1, 2) if j < 9 else (2,)):
            jobs.append((j, g, "K"))
            jobs.append((j, g, "V"))
    wq = {}

    def wload(idx):
        if idx < len(jobs) and idx not in wq:
            j_, g_, kind_ = jobs[idx]
            wq[idx] = load_w(wb_in, "in", (COL_K if kind_ == "K" else COL_VA) + 512 * g_, 512)

    XNTB = [av(50 + kc, KB, BF16) for kc in range(8)]
    XNTB3 = av(50, 8 * KB, BF16)
    XHB = [av(58 + 4 * tb, 4 * KB, F32) for tb in range(4)]
    XHB_all = av(58, 16 * KB, F32)
    xbuf = [(XH, XH_all, "x"), (XHB, XHB_all, "xb")]
    nbuf = [(XNT3, XNT), (XNTB3, XNTB)]
    NT1 = NTOK // T
    XS4 = [av(74 + 2 * i, 2 * KB, BF16) for i in range(4)]
    KSTS = [KST, av(82, 4 * KB, BF16)]
    kst_i = [0]
    load_x(0, xbuf[0][1], xbuf[0][2])
    wload(0)
    wload(1)
    norm_T(GMIX, xbuf[0][0], DST=nbuf[0])
    cur_tile = -1
    job_in_tile = 0
    for idx, (j, g, kind) in enumerate(jobs):
        if j != cur_tile:
            cur_tile = j
            job_in_tile = 0
            if j + 1 < NT1:
                load_x(j + 1, xbuf[(j + 1) % 2][1], xbuf[(j + 1) % 2][2])
            issue_conv(8, ffn=False)
        XNTc = nbuf[j % 2][1]
        if j + 1 < NT1:
            SRCn = xbuf[(j + 1) % 2][0]
            if job_in_tile == 1:
                for tb in range(4):
                    rstd_of(SRCn[tb], tb, 1.0 / D)
                for tb in range(4):
                    norm_scale(GMIX, SRCn, tb, XS4[tb])
            if job_in_tile == 4:
                for tb in range(4):
                    norm_xpose(tb, XS4[tb], nbuf[(j + 1) % 2][0], nbuf[(j + 1) % 2][1])
        job_in_tile += 1
        wsv, wap = wq.pop(idx)
        wload(idx + 1)
        wload(idx + 2)
        if kind == "K":
            KSTc = KSTS[kst_i[0] % 2]
            kst_s = "kst%d" % (kst_i[0] % 2)
            kst_i[0] += 1
            for c in range(4):
                b = nb()
                for kc in range(8):
                    tk.op("pe", lambda e: e.matmul(PS(b), wap[:, kc, c * 128:(c + 1) * 128], XNTc[kc].ap,
                                                   start=(kc == 0), stop=(kc == 7)),
                          R=wsv.keys + XNTc[kc].keys, W=PK(b), sig=(kc == 7))
                tk.op("act", lambda e: e.activation(out=KSTc.ap[:, c * 512:(c + 1) * 512], in_=PS(b), func=AF.Copy),
                      R=PK(b), W=KSTc.keys)
            tk.dma("act", kscr[g, :, :, PAD + j * T:PAD + (j + 1) * T].rearrange("c p t -> p c t"),
                   v3(KSTc, 4), R=KSTc.keys, W=[("dr", "kv", 6 + (kst_i[0] - 1) % 2)], stream=kst_s)
        else:
            for tb in range(4):
                b = nb()
                for kc in range(8):
                    tk.op("pe", lambda e: e.matmul(PS(b), XNTc[kc].ap[:, tb * 128:(tb + 1) * 128], wap[:, kc, :],
                                                   start=(kc == 0), stop=(kc == 7)),
                          R=wsv.keys + XNTc[kc].keys, W=PK(b), sig=(kc == 7))
                vs = VSTG[tb]
                tk.op("dve", lambda e: e.tensor_copy(out=v3(vs, 8)[:, :, 0:64],
                                                     in_=ps[:, b, :].rearrange("p (h e) -> p h e", h=8)),
                      R=PK(b), W=vs.keys)
                r0 = PAD + j * T + tb * 128
                tk.dma("act", vscr[g, r0:r0 + 128, :], vs.ap, R=vs.keys, W=KV_KEYS[2 + tb:3 + tb], stream="vst%d" % tb)

    issue_conv(len(conv_jobs), ffn=False)
    issue_conv(len(conv_jobs))
    ST6 = [small(12, "st6_%d" % i) for i in range(2)]
    MV = [small(2, "mv%d" % i) for i in range(2)]
    pvb_i = [0]
    scb_i = [0]
    pt_i = [0]
    sbf_i = [0]

    def fm_proj(dram, name, c0, rhs_list, nk, epilogue, krows=D, k64=False):
        for half in range(2):
            wsv, wap = load_w(dram, name, c0 + 512 * half, 512, krows=krows)
            for q in range(4):
                oc = half * 4 + q
                b = nb()
                for kc in range(nk):
                    rv = rhs_list[kc]
                    if k64:
                        lhs = wap[0:64, kc, q * 128:(q + 1) * 128]
                        rhs = rv.ap[0:64, :]
                    else:
                        lhs = wap[:, kc, q * 128:(q + 1) * 128]
                        rhs = rv.ap
                    tk.op("pe", lambda e: e.matmul(PS(b), lhs, rhs, start=(kc == 0), stop=(kc == nk - 1)),
                          R=wsv.keys + rv.keys, W=PK(b), sig=(kc == nk - 1))
                epilogue(oc, b)

    for j in range(NOWN // T):
        if j == 0:
            load_x(0, XP_all, "xp")
            norm_T(GMIX, XP)

        wv = [load_w(wb_in, "in", COL_V + 512 * hf, 512) for hf in range(2)]
        for tb in range(4):
            vst = VST[tb % 2]
            st6 = ST6[tb % 2]
            mv = MV[tb % 2]
            for hf in range(2):
                b = nb()
                wsv, wap = wv[hf]
                for kc in range(8):
                    tk.op("pe", lambda e: e.matmul(PS(b), XNT[kc].ap[:, tb * 128:(tb + 1) * 128], wap[:, kc, :],
                                                   start=(kc == 0), stop=(kc == 7)),
                          R=wsv.keys + XNT[kc].keys, W=PK(b), sig=(kc == 7))
                tk.op("act", lambda e: e.activation(out=vst.ap[:, hf * 512:(hf + 1) * 512], in_=PS(b), func=AF.Gelu),
                      R=PK(b), W=vst.keys)
                tk.op("dve", lambda e: e.bn_stats(out=st6.ap[:, hf * 6:(hf + 1) * 6], in_=vst.ap[:, hf * 512:(hf + 1) * 512]),
                      R=vst.keys, W=st6.keys)
            tk.op("dve", lambda e: e.bn_aggr(out=mv.ap, in_=st6.ap), R=st6.keys, W=mv.keys)
            tk.op("dve", lambda e: e.tensor_scalar(out=TA[tb].ap, in0=mv.ap[:, 1:2], scalar1=EPS, scalar2=None, op0=ALU.add),
                  R=mv.keys, W=TA[tb].keys)
            pe_ = "dve" if j == 0 else "pool"
            if j == 0:
                tk.op("act", lambda e: e.activation(out=TB[tb].ap, in_=TA[tb].ap, func=AF.Sqrt), R=TA[tb].keys, W=TB[tb].keys)
                tk.op("dve", lambda e: e.reciprocal(out=RS[tb].ap, in_=TB[tb].ap), R=TB[tb].keys, W=RS[tb].keys)
            else:
                tk.op("pool", lambda e: e.tensor_tensor(out=RS[tb].ap, in0=TA[tb].ap, in1=MHALF.ap, op=ALU.pow),
                      R=TA[tb].keys + MHALF.keys, W=RS[tb].keys)
            tk.op("dve", lambda e: e.tensor_scalar(out=vst.ap, in0=vst.ap, scalar1=mv.ap[:, 0:1], scalar2=RS[tb].ap,
                                                   op0=ALU.subtract, op1=ALU.mult),
                  R=vst.keys + mv.keys + RS[tb].keys, W=vst.keys)
            tk.op(pe_, lambda e: e.tensor_tensor(out=vst.ap, in0=vst.ap, in1=LG.ap, op=ALU.mult),
                  R=vst.keys + LG.keys, W=vst.keys)
            tk.op(pe_, lambda e: e.tensor_tensor(out=VN[tb].ap, in0=vst.ap, in1=LB.ap, op=ALU.add),
                  R=vst.keys + LB.keys, W=VN[tb].keys)

        def ep_u(oc, b):
            tk.op("act", lambda e: e.activation(out=GU[oc].ap, in_=PS(b), func=AF.Gelu), R=PK(b), W=GU[oc].keys)
        fm_proj(wb_in, "in", COL_U, XNT, 8, ep_u)

        def spatial_group(g):
            b = nb()
            for tb in range(4):
                tk.op("pe", lambda e: e.matmul(ps[:, b, tb * 128:(tb + 1) * 128], VN[tb].ap[:, g * 128:(g + 1) * 128],
                                               WST.ap[:, g * 128:(g + 1) * 128], start=True, stop=True),
                      R=VN[tb].keys + WST.keys, W=PK(b), sig=(tb == 3))
            t1 = T12[g % 2]
            bsap = BSB.ap[:, g * 128:(g + 1) * 128].unsqueeze(1).broadcast_to([128, 4, 128])
            tk.op("dve", lambda e: e.tensor_tensor(out=v3(t1, 4), in0=ps[:, b, :].rearrange("p (a b) -> p a b", a=4),
                                                   in1=bsap, op=ALU.add),
                  R=PK(b) + BSB.keys, W=t1.keys)
            tk.op("dve" if j == 0 else "pool", lambda e: e.tensor_tensor(out=GU[g].ap, in0=t1.ap, in1=GU[g].ap, op=ALU.mult),
                  R=t1.keys + GU[g].keys, W=GU[g].keys)

        def ep_ga(oc, b):
            tk.op("act", lambda e: e.activation(out=SA[oc].ap, in_=PS(b), func=AF.Sigmoid), R=PK(b), W=SA[oc].keys)
            spatial_group(oc)
        fm_proj(wb_in, "in", COL_GA, XNT, 8, ep_ga)

        def ep_m1(oc, b):
            tk.op("dve", lambda e: e.tensor_tensor(out=M1[oc].ap, in0=PS(b), in1=SA[oc].ap, op=ALU.mult),
                  R=PK(b) + SA[oc].keys, W=M1[oc].keys)
        fm_proj(wb_a, "a", 0, GU, 8, ep_m1)

        for g in range(3):
            d = DIL[g]
            qt = QT[g % 2]
            wsv, wap = load_w(wb_in, "in", COL_Q + 512 * g, 512)
            for c in range(4):
                b = nb()
                for kc in range(8):
                    tk.op("pe", lambda e: e.matmul(PS(b), wap[:, kc, c * 128:(c + 1) * 128], XNT[kc].ap,
                                                   start=(kc == 0), stop=(kc == 7)),
                          R=wsv.keys + XNT[kc].keys, W=PK(b), sig=(kc == 7))
                for v_ in range(2):
                    tk.op("dve", lambda e: e.tensor_scalar(out=qt[c][v_].ap, in0=PS(b), scalar1=SEO[v_].ap, scalar2=None,
                                                           op0=ALU.mult),
                          R=PK(b) + SEO[v_].keys, W=qt[c][v_].keys)
            Wg = T + 128 * d
            wstart = PAD + j * T - 64 * d
            KWg = KW1 if g == 1 else KW
            kw3 = KWg.ap[:, 0:4 * Wg].rearrange("p (c t) -> p c t", c=4)
            tk.dma("sp", kw3, kscr[g, :, :, wstart:wstart + Wg].rearrange("c p t -> p c t"),
                   R=KV_KEYS, W=KWg.keys, stream=("kw1" if g == 1 else "kw"))

            base = PAD + j * T

            def mk_units(pi):
                units = []
                if g == 0:
                    tk.dma("sp", VA_all.ap[:, 0:5 * 520].rearrange("p (m c) -> p m c", m=5),
                           vscr[g, base - 64:base - 64 + 640, :].rearrange("(m p) c -> p m c", p=128),
                           R=KV_KEYS, W=VA_all.keys, stream="va_a")
                    for qb in range(4):
                        units.append(dict(q0=128 * qb, qs=1, nq=128,
                                          tiles=[(128 * qb, 1, 128, VA[qb]), (128 * qb + 128, 1, 128, VA[qb + 1])]))
                elif g == 1:
                    ka = []
                    kb = []
                    for i_ in range(4):
                        ka += VA[i_].keys
                        kb += VA[4 + i_].keys
                    tk.dma("sp", VA_all.ap[:, 0:4 * 520],
                           vscr[g, base - 256:base - 256 + 512, :].rearrange("(p r) c -> p (r c)", r=4),
                           R=KV_KEYS, W=ka, stream="va_a")
                    tk.dma("sp", VA_all.ap[:, 4 * 520:8 * 520],
                           vscr[g, base + 256:base + 256 + 512, :].rearrange("(p r) c -> p (r c)", r=4),
                           R=KV_KEYS, W=kb, stream="va_b")
                    for r in range(4):
                        units.append(dict(q0=r, qs=4, nq=128, tiles=[(r, 4, 128, VA[r]), (r + 512, 4, 128, VA[4 + r])]))
                else:
                    tk.dma("sp", VA_all.ap.rearrange("p (r c) -> p r c", r=8),
                           vscr[g, base - 1024:base - 1024 + 2048, :].rearrange("(p r) c -> p r c", r=16)[:, 8 * pi:8 * pi + 8, :],
                           R=KV_KEYS, W=VA_all.keys, stream="va_a")
                    tk.dma("sp", VB_all.ap[0:32, :].rearrange("p (r c) -> p r c", r=8),
                           vscr[g, base + 1024:base + 1024 + 512, :].rearrange("(p r) c -> p r c", r=16)[:, 8 * pi:8 * pi + 8, :],
                           R=KV_KEYS, W=VB_all.keys, stream="vb")
                    for rr in range(8):
                        r = 8 * pi + rr
                        units.append(dict(q0=r, qs=16, nq=32, tiles=[(r, 16, 128, VA[rr]), (r + 2048, 16, 32, VB[rr])]))
                return units

            for pi in range(2 if g == 2 else 1):
                units = mk_units(pi)
                blist = []
                for h in range(8):
                    bl = [units[0:2], units[2:4]] if g < 2 else [units]
                    for bi, bu in enumerate(bl):
                        blist.append((h, bu, bi == 0, bi == len(bl) - 1))
                bstate = {}
                pvbank = {}

                def stage_S(k):
                    h, bu, first, last = blist[k]
                    c = h // 2
                    po = 64 * (h % 2)
                    if first:
                        pvbank[h] = pvb_i[0] % 3
                        pvb_i[0] += 1
                    sbk = 3 + scb_i[0] % 5
                    scb_i[0] += 1
                    col = 0
                    segs = []
                    if g < 2:
                        for u in bu:
                            for ti in range(2):
                                segs.append((u, ti, col, u["tiles"][ti][2], u["nq"]))
                                col += u["nq"]
                    else:
                        for u in bu:
                            segs.append((u, 0, col, 128, 32))
                            col += 32
                        for u in bu:
                            segs.append((u, 1, col, 32, 32))
                            col += 32
                    for si, (u, ti, c0, nk, nq) in enumerate(segs):
                        k0, ks, nk_, vt_ = u["tiles"][ti]
                        lhs = kw3[:, c, sl(k0, nk, ks)]
                        qv = qt[c][h % 2]
                        rhs = qv.ap[:, sl(u["q0"], nq, u["qs"])]
                        tk.op("pe", lambda e: e.matmul(ps[0:nk, sbk, c0:c0 + nq], lhs, rhs, start=True, stop=True),
                              R=KWg.keys + qv.keys, W=PK(sbk), sig=(si == len(segs) - 1))
                    sbf = SBF[sbf_i[0] % 2]
                    sbf_i[0] += 1
                    ptv = (PT2 if g == 2 else PT)[pt_i[0] % 3]
                    pt_i[0] += 1
                    if g < 2:
                        o_ = (g * 8 + h) * 256
                        bt = BT01.ap[:, o_:o_ + 256].unsqueeze(1).broadcast_to([128, 2, 256])
                        tk.op("dve", lambda e: e.tensor_tensor(out=sbf.ap.rearrange("p (a b) -> p a b", a=2),
                                                               in0=ps[:, sbk, :].rearrange("p (a b) -> p a b", a=2),
                                                               in1=bt, op=ALU.add),
                              R=PK(sbk) + BT01.keys, W=sbf.keys)
                        tk.op("act", lambda e: e.activation(out=ptv.ap, in_=sbf.ap, func=AF.Exp), R=sbf.keys, W=ptv.keys)
                    else:
                        btA = BT2.ap[:, h * 64:h * 64 + 32].unsqueeze(1).broadcast_to([128, 8, 32])
                        btB = BT2.ap[0:32, h * 64 + 32:h * 64 + 64].unsqueeze(1).broadcast_to([32, 8, 32])
                        tk.op("dve", lambda e: e.tensor_tensor(out=sbf.ap[:, 0:256].rearrange("p (a b) -> p a b", a=8),
                                                               in0=ps[:, sbk, 0:256].rearrange("p (a b) -> p a b", a=8),
                                                               in1=btA, op=ALU.add),
                              R=PK(sbk) + BT2.keys, W=sbf.keys)
                        tk.op("dve", lambda e: e.tensor_tensor(out=sbf.ap[0:32, 256:512].rearrange("p (a b) -> p a b", a=8),
                                                               in0=ps[0:32, sbk, 256:512].rearrange("p (a b) -> p a b", a=8),
                                                               in1=btB, op=ALU.add),
                              R=PK(sbk) + BT2.keys, W=sbf.keys)
                        tk.op("act", lambda e: e.activation(out=ptv.ap[:, 0:256], in_=sbf.ap[:, 0:256], func=AF.Exp),
                              R=sbf.keys, W=ptv.keys)
                        tk.op("act", lambda e: e.activation(out=ptv.ap[0:32, 256:512], in_=sbf.ap[0:32, 256:512], func=AF.Exp),
                              R=sbf.keys, W=ptv.keys)
                    bstate[k] = (segs, ptv)

                def stage_P(k):
                    h, bu, first, last = blist[k]
                    segs, ptv = bstate.pop(k)
                    pv = pvbank[h]
                    n = len(segs)
                    order = sorted(range(n), key=lambda i: (segs[i][1], segs[i][0]["q0"]))
                    for oi, i in enumerate(order):
                        u, ti, c0, nk, nq = segs[i]
                        k0, ks, nk_, vt_ = u["tiles"][ti]
                        nke = 128 if g == 2 else nk
                        lhs = vt_.ap[0:nke, :].rearrange("p (h e) -> p h e", h=8)[:, h, :]
                        rhs = ptv.ap[0:nke, c0:c0 + nq]
                        outap = ps[0:65, pv, sl(u["q0"], nq, u["qs"])]
                        st_ = (first and oi == 0)
                        tk.op("pe", lambda e: e.matmul(outap, lhs, rhs, start=st_, stop=(last and oi == n - 1),
                                                       skip_group_check=True),
                              R=vt_.keys + ptv.keys, W=PK(pv), sig=(oi == n - 1))
                    if not last:
                        return
                    if g == 0:
                        tk.op("act", lambda e: e.activation(out=ACC[h].ap[0:65, :], in_=ps[0:65, pv, :], func=AF.Copy),
                              R=PK(pv), W=ACC[h].keys)
                    elif g == 1:
                        tk.op("dve", lambda e: e.tensor_tensor(out=ACC[h].ap[0:65, :], in0=ps[0:65, pv, :], in1=ACC[h].ap[0:65, :],
                                                               op=ALU.add),
                              R=PK(pv) + ACC[h].keys, W=ACC[h].keys)
                    else:
                        a3 = ACC[h].ap[0:65, :].rearrange("p (i r) -> p i r", r=16)[:, :, 8 * pi:8 * pi + 8]
                        p3 = ps[0:65, pv, :].rearrange("p (i r) -> p i r", r=16)[:, :, 8 * pi:8 * pi + 8]
                        tk.op("dve", lambda e: e.tensor_tensor(out=a3, in0=p3, in1=a3, op=ALU.add),
                              R=PK(pv) + ACC[h].keys, W=ACC[h].keys)

                nbat = len(blist)
                for k in range(nbat + LA):
                    if k < nbat:
                        stage_S(k)
                    if k >= LA:
                        stage_P(k - LA)

        acc_keys = []
        for h_ in range(8):
            acc_keys += ACC[h_].keys
        den_all = av(42, 16 * KB, F32).ap[64:65, :]
        tk.op("act", lambda e: e.activation(out=den_all, in_=den_all, func=AF.Ln), R=acc_keys, W=acc_keys)
        tk.op("act", lambda e: e.activation(out=den_all, in_=den_all, func=AF.Exp, scale=-1.0), R=acc_keys, W=acc_keys)

        def bc_head(h):
            b = nb()
            tk.op("pe", lambda e: e.matmul(ps[0:64, b, :], ONE.ap[64:65, 0:64], ACC[h].ap[64:65, :], start=True, stop=True),
                  R=ONE.keys + ACC[h].keys, W=PK(b))
            tk.op("dve", lambda e: e.tensor_tensor(out=YAT[h].ap[0:64, :], in0=ps[0:64, b, :], in1=ACC[h].ap[0:64, :], op=ALU.mult),
                  R=PK(b) + ACC[h].keys, W=YAT[h].keys)

        def ep_gb(oc, b):
            tk.op("act", lambda e: e.activation(out=SBG[oc].ap, in_=PS(b), func=AF.Sigmoid), R=PK(b), W=SBG[oc].keys)
        fm_proj(wb_in, "in", COL_GB, XNT, 8, ep_gb)
        for h_ in range(8):
            bc_head(h_)


        def ep_mt(oc, b):
            t2 = T12[2]
            tk.op("dve", lambda e: e.tensor_tensor(out=t2.ap, in0=PS(b), in1=SBG[oc].ap, op=ALU.mult),
                  R=PK(b) + SBG[oc].keys, W=t2.keys)
            tk.op("dve" if j == 0 else "pool", lambda e: e.tensor_tensor(out=MT[oc].ap, in0=t2.ap, in1=M1[oc].ap, op=ALU.add),
                  R=t2.keys + M1[oc].keys, W=MT[oc].keys)
        for half in range(2):
            s = ws_i[0] % 3
            ws_i[0] += 1
            dst = WS[s].ap[0:64, 0:8 * 512].rearrange("p (k c) -> p k c", k=8)
            src = wb_b[:, 512 * half:512 * half + 512].rearrange("(k p) c -> p k c", p=64)
            tk.dma("sp", dst, src, R=[("w", "b", half)], W=WS[s].keys, stream="w%d" % s)
            for q in range(4):
                oc = half * 4 + q
                b = nb()
                for hh in range(8):
                    tk.op("pe", lambda e: e.matmul(PS(b), dst[:, hh, q * 128:(q + 1) * 128], YAT[hh].ap[0:64, :],
                                                   start=(hh == 0), stop=(hh == 7)),
                          R=WS[s].keys + YAT[hh].keys, W=PK(b), sig=(hh == 7))
                ep_mt(oc, b)

        wo = [load_w(wb_out, "out", 512 * hf, 512) for hf in range(2)]
        load_x(j)
        if j + 1 < NOWN // T:
            load_x(j + 1, XP_all, "xp")
        for tb in range(4):
            for hf in range(2):
                b = nb()
                wsv, wap = wo[hf]
                for kc in range(8):
                    tk.op("pe", lambda e: e.matmul(PS(b), MT[kc].ap[:, tb * 128:(tb + 1) * 128], wap[:, kc, :],
                                                   start=(kc == 0), stop=(kc == 7)),
                          R=wsv.keys + MT[kc].keys, W=PK(b), sig=(kc == 7))
                xa = XH[tb].ap[:, hf * 512:(hf + 1) * 512]
                tk.op("dve", lambda e: e.tensor_tensor(out=xa, in0=PS(b), in1=xa, op=ALU.add),
                      R=PK(b) + XH[tb].keys, W=XH[tb].keys)
            rstd_of(XH[tb], tb, 1.0 / D)

        for tb in range(4):
            norm_scale(GFFN, XH, tb, XS[tb % 2])
            norm_xpose(tb, XS[tb % 2], XNT3, XNT)
        use_pool_pow[0] = True
        def load_gu(fc):
            s_ = ws_i[0] % 3
            ws_i[0] += 1
            dstg = WS[s_].ap[:, 0:2048].rearrange("p (k c) -> p k c", k=8)
            dstu = WS[s_].ap[:, 2048:4096].rearrange("p (k c) -> p k c", k=8)
            cg = (fc * 128) // 512
            tk.dma("sp", dstg, wb_gate[:, fc * 128:fc * 128 + 256].rearrange("(k p) c -> p k c", p=128),
                   R=[("w", "gate", cg)], W=WS[s_].keys, stream="w%d" % s_)
            tk.dma("sp", dstu, wb_up[:, fc * 128:fc * 128 + 256].rearrange("(k p) c -> p k c", p=128),
                   R=[("w", "up", cg)], W=WS[s_].keys, stream="w%d" % s_)
            return WS[s_], dstg, dstu

        gu = {0: load_gu(0), 2: load_gu(2)}
        for fc in range(0, NFC, 2):
            wsv, wgap, wuap = gu.pop(fc)
            if fc + 4 < NFC:
                gu[fc + 4] = load_gu(fc + 4)
            for q in range(2):
                bg = nb()
                bu = nb()
                for (bb, wap) in ((bg, wgap), (bu, wuap)):
                    for kc in range(8):
                        tk.op("pe", lambda e: e.matmul(PS(bb), wap[:, kc, q * 128:(q + 1) * 128], XNT[kc].ap,
                                                       start=(kc == 0), stop=(kc == 7)),
                              R=wsv.keys + XNT[kc].keys, W=PK(bb), sig=(kc == 7))
                sg = SG[(fc + q) % 2]
                tk.op("act", lambda e: e.activation(out=sg.ap, in_=PS(bg), func=AF.Silu), R=PK(bg), W=sg.keys)
                tk.op("dve", lambda e: e.tensor_tensor(out=AT[fc + q].ap, in0=PS(bu), in1=sg.ap, op=ALU.mult),
                      R=PK(bu) + sg.keys, W=AT[fc + q].keys)
        for hf in range(2):
            banks = [nb() for _ in range(4)]
            f0 = 0
            while f0 < NFC:
                n = min(8, NFC - f0)
                s = ws_i[0] % 3
                ws_i[0] += 1
                dst = WS[s].ap[:, 0:n * 512].rearrange("p (k c) -> p k c", k=n)
                src = wb_down[f0 * 128:(f0 + n) * 128, hf * 512:(hf + 1) * 512].rearrange("(k p) c -> p k c", p=128)
                tk.dma("sp", dst, src, R=[("w", "down", hf)], W=WS[s].keys, stream="w%d" % s)
                for tb in range(4):
                    for q in range(n):
                        f = f0 + q
                        tk.op("pe", lambda e: e.matmul(PS(banks[tb]), AT[f].ap[:, tb * 128:(tb + 1) * 128], dst[:, q, :],
                                                       start=(f == 0), stop=(f == NFC - 1)),
                              R=WS[s].keys + AT[f].keys, W=PK(banks[tb]), sig=(q == n - 1))
                f0 += n
            if hf == 1 and j + 1 < NOWN // T:
                norm_T(GMIX, XP, tbs=(0, 1))
            for tb in range(4):
                xa = XH[tb].ap[:, hf * 512:(hf + 1) * 512]
                tk.op("dve", lambda e: e.tensor_tensor(out=xa, in0=PS(banks[tb]), in1=xa, op=ALU.add),
                      R=PK(banks[tb]) + XH[tb].keys, W=XH[tb].keys)
                if hf == 1:
                    rstd_of(XH[tb], 4 + tb, 1.0 / D)
            if hf == 1 and j + 1 < NOWN // T:
                norm_T(GMIX, XP, tbs=(2, 3))

        for tb in range(4):
            tk.op("dve", lambda e: e.scalar_tensor_tensor(out=XH[tb].ap, in0=XH[tb].ap, scalar=RS[4 + tb].ap, in1=GFIN.ap,
                                                          op0=ALU.mult, op1=ALU.mult),
                  R=XH[tb].keys + RS[4 + tb].keys + GFIN.keys, W=XH[tb].keys)
        tk.dma("act", y_d[j * T:(j + 1) * T, :].rearrange("(tb p) f -> p tb f", p=128), v3(XH_all, 4),
               R=XH_all.keys, stream="out")

    ko = tk.stream("out")
    nc.sync.wait_ge(tk.sem[ko], tk.cnt[ko])
    return nc


def _bias_tables():
    n = 24
    slopes = np.exp2(-8.0 * np.arange(1, n + 1, dtype=np.float64) / n).reshape(3, 8)
    kk = np.arange(128)[:, None]
    bt01 = np.zeros((128, 2, 8, 256), np.float32)
    for g in range(2):
        for h in range(8):
            for t, sh in enumerate((-64, 64)):
                i = np.arange(128)[None, :]
                rel = kk + sh - i
                b = np.where(np.abs(rel) <= 64, -slopes[g, h] * DIL[g] * np.abs(rel), NEG)
                bt01[:, g, h, t * 128:(t + 1) * 128] = b
    bt2 = np.full((128, 8, 64), NEG, np.float32)
    i = np.arange(32)[None, :]
    for h in range(8):
        rel = kk - 64 - i
        bt2[:, h, 0:32] = np.where(np.abs(rel) <= 64, -slopes[2, h] * 16 * np.abs(rel), NEG)
        rel = kk + 64 - i
        bt2[:, h, 32:64] = np.where(np.abs(rel) <= 64, -slopes[2, h] * 16 * np.abs(rel), NEG)
    return bt01.reshape(128, -1), bt2.reshape(128, -1)


_NC_CACHE = {}


def kernel(x, norm_mix_g, w_in, gmlp_ln_g, gmlp_ln_b, gmlp_ws, gmlp_bs, w_branch_gmlp,
           w_branch_attn, w_out, norm_ffn_g, w_ffn_gate, w_ffn_up, w_ffn_down, norm_final_g):
    f = lambda a: np.ascontiguousarray(np.asarray(a, dtype=np.float32))
    x = f(x)
    if "nc" not in _NC_CACHE:
        _NC_CACHE["nc"] = build_program()
    nc = _NC_CACHE["nc"]
    bt01, bt2 = _bias_tables()
    idn = np.eye(128, dtype=np.float32)
    ws = f(gmlp_ws)[0]
    bs = f(gmlp_bs)[0]
    common = {
        "w_in": f(w_in)[0], "w_a": f(w_branch_gmlp)[0], "w_b": f(w_branch_attn)[0], "w_out": f(w_out)[0],
        "w_gate": f(w_ffn_gate)[0], "w_up": f(w_ffn_up)[0], "w_down": f(w_ffn_down)[0],
        "gmix": f(norm_mix_g).reshape(1, D), "gffn": f(norm_ffn_g).reshape(1, D), "gfin": f(norm_final_g).reshape(1, D),
        "lng": f(gmlp_ln_g).reshape(1, D), "lnb": f(gmlp_ln_b).reshape(1, D),
        "bt01": bt01, "bt2": bt2, "idn": idn,
    }
    in_maps = []
    for c in range(8):
        b, half = c // 2, c % 2
        if half == 0:
            xs = x[b, 0:NTOK]
            wsl, bsl = ws, bs
        else:
            xs = x[b, SEQ - NTOK:SEQ][::-1]
            wsl, bsl = ws[:, ::-1, ::-1], bs[:, ::-1]
        m = dict(common)
        m["x"] = np.ascontiguousarray(xs)
        m["wst"] = np.ascontiguousarray(np.transpose(wsl, (2, 0, 1)).reshape(128, 1024))
        m["bs"] = np.ascontiguousarray(bsl.reshape(1, D))
        in_maps.append(m)
    res = run_bass_kernel_spmd(nc, in_maps, core_ids=list(range(8)))
    out = np.empty((BATCH, SEQ, D), np.float32)
    for c in range(8):
        b, half = c // 2, c % 2
        yc = np.asarray(res.results[c]["y"], dtype=np.float32)
        if half == 0:
            out[b, 0:NOWN] = yc
        else:
            out[b, NOWN:SEQ] = yc[::-1]
    return out
```
